# Optimizing a Trainium2 kernel written in Bass

```python
import math
import jax, jax.numpy as jnp
from jax import lax
import numpy as np

D_MODEL = 1024
BATCH = 2
SEQ = 8192
DEPTH = 1

CHUNK = 64
N_PREV_CHUNKS = 8
BAND = (N_PREV_CHUNKS + 1) * CHUNK
HEAD_DIM = 64
H_A = 8
H_B = 4
V_DIM_B = 2 * HEAD_DIM
W_A = H_A * HEAD_DIM
QK_B = H_B * 2 * HEAD_DIM
W_B = H_B * V_DIM_B
REL_CLIP = 128
N_REL = 2 * REL_CLIP + 1
ROT_DIM = HEAD_DIM // 4
ROPE_THETA = 500000.0
D_FF = 4 * D_MODEL
Q_BLOCK = 128
LN_EPS = 1e-5
RMS_EPS = 1e-5
NEG_INF = -1e30
DEEPNORM_ALPHA = (2.0 * DEPTH) ** 0.25
DEEPNORM_BETA = (8.0 * DEPTH) ** -0.25
N_GATES = 2
IN_COLS = 3 * W_A + 2 * QK_B + W_B + N_GATES * D_MODEL

kernel_name = "hybrid_chunkattn_diffattn_deepnorm"


def layer_norm(x, g, b):
    xf = x.astype(jnp.float32)
    mu = jnp.mean(xf, axis=-1, keepdims=True)
    var = jnp.mean(jnp.square(xf - mu), axis=-1, keepdims=True)
    y = (xf - mu) * lax.rsqrt(var + LN_EPS)
    return (y * g.astype(jnp.float32) + b.astype(jnp.float32)).astype(x.dtype)


def rms_norm(x, g):
    xf = x.astype(jnp.float32)
    y = xf * lax.rsqrt(jnp.mean(jnp.square(xf), axis=-1, keepdims=True) + RMS_EPS)
    return (y * g.astype(jnp.float32)).astype(x.dtype)


def rope_tables(positions):
    inv_freq = ROPE_THETA ** (-jnp.arange(0, ROT_DIM, 2, dtype=jnp.float32) / ROT_DIM)
    ang = positions.astype(jnp.float32)[..., None] * inv_freq
    return jnp.cos(ang)[:, :, None, :], jnp.sin(ang)[:, :, None, :]


def partial_rope(t, cos, sin):
    half = ROT_DIM // 2
    tf = t.astype(jnp.float32)
    t1 = tf[..., :half]
    t2 = tf[..., half:ROT_DIM]
    out = jnp.concatenate([t1 * cos - t2 * sin, t1 * sin + t2 * cos, tf[..., ROT_DIM:]], axis=-1)
    return out.astype(t.dtype)


def chunked_relbias_attention(q, k, v, rel_bias):
    b, s, h, d = q.shape
    nc = s // CHUNK
    qc = q.reshape(b, nc, CHUNK, h, d)
    pad = ((0, 0), (N_PREV_CHUNKS, 0), (0, 0), (0, 0), (0, 0))
    kp = jnp.pad(k.reshape(b, nc, CHUNK, h, d), pad)
    vp = jnp.pad(v.reshape(b, nc, CHUNK, h, d), pad)
    k_band = jnp.concatenate([kp[:, j:j + nc] for j in range(N_PREV_CHUNKS + 1)], axis=2)
    v_band = jnp.concatenate([vp[:, j:j + nc] for j in range(N_PREV_CHUNKS + 1)], axis=2)
    dist = (jnp.arange(CHUNK)[:, None] + N_PREV_CHUNKS * CHUNK) - jnp.arange(BAND)[None, :]
    idx = jnp.clip(dist, -REL_CLIP, REL_CLIP) + REL_CLIP
    bias = rel_bias.astype(jnp.float32)[:, idx]
    key_abs = jnp.arange(nc)[:, None] * CHUNK - N_PREV_CHUNKS * CHUNK + jnp.arange(BAND)[None, :]
    valid = (key_abs >= 0)[None, :, None, None, :]
    scores = jnp.einsum("bcqhd,bckhd->bchqk", qc, k_band).astype(jnp.float32) / math.sqrt(d)
    scores = jnp.where(valid, scores + bias[None, None], NEG_INF)
    probs = jax.nn.softmax(scores, axis=-1).astype(v.dtype)
    out = jnp.einsum("bchqk,bckhd->bcqhd", probs, v_band)
    return out.reshape(b, s, h * d)


def differential_attention(q, k, v, lam, lam_init, subln_g):
    b, s, h, _, d = q.shape
    nb = s // Q_BLOCK
    q1 = jnp.transpose(q[:, :, :, 0], (0, 2, 1, 3))
    q2 = jnp.transpose(q[:, :, :, 1], (0, 2, 1, 3))
    k1 = jnp.transpose(k[:, :, :, 0], (0, 2, 1, 3))
    k2 = jnp.transpose(k[:, :, :, 1], (0, 2, 1, 3))
    vt = jnp.transpose(v, (0, 2, 1, 3))
    q1b = jnp.transpose(q1.reshape(b, h, nb, Q_BLOCK, d), (2, 0, 1, 3, 4))
    q2b = jnp.transpose(q2.reshape(b, h, nb, Q_BLOCK, d), (2, 0, 1, 3, 4))
    key_chunk = jnp.arange(s) // CHUNK
    scale = 1.0 / math.sqrt(d)

    def block(args):
        qa, qb, bi = args
        q_chunk = (bi * Q_BLOCK + jnp.arange(Q_BLOCK)) // CHUNK
        mask = key_chunk[None, :] <= q_chunk[:, None]
        s1 = jnp.einsum("bhqd,bhkd->bhqk", qa, k1).astype(jnp.float32) * scale
        s2 = jnp.einsum("bhqd,bhkd->bhqk", qb, k2).astype(jnp.float32) * scale
        p1 = jax.nn.softmax(jnp.where(mask, s1, NEG_INF), axis=-1)
        p2 = jax.nn.softmax(jnp.where(mask, s2, NEG_INF), axis=-1)
        w = (p1 - lam * p2).astype(vt.dtype)
        return jnp.einsum("bhqk,bhkd->bhqd", w, vt)

    out = lax.map(block, (q1b, q2b, jnp.arange(nb)))
    out = jnp.transpose(out, (1, 3, 0, 2, 4)).reshape(b, s, h, V_DIM_B)
    out = rms_norm(out, subln_g) * (1.0 - lam_init)
    return out.reshape(b, s, h * V_DIM_B)


def setup_inputs(seed: int = 0) -> dict:
    key = jax.random.key(seed)
    ks = jax.random.split(key, 20)

    def nrm(k, shape, scale):
        return jax.random.normal(k, shape, jnp.float32) * scale

    x = nrm(ks[0], (BATCH, SEQ, D_MODEL), 1.0)
    offsets = jax.random.randint(ks[1], (BATCH, 1), 0, 64) * CHUNK
    positions = (offsets + jnp.arange(SEQ, dtype=jnp.int32)[None, :]).astype(jnp.int32)
    return {
        "x": x,
        "positions": positions,
        "w_in": nrm(ks[2], (DEPTH, D_MODEL, IN_COLS), D_MODEL ** -0.5),
        "b_gate": nrm(ks[3], (DEPTH, N_GATES * D_MODEL), 0.1),
        "rel_bias": nrm(ks[4], (DEPTH, H_A, N_REL), 0.5),
        "lambda_q1": nrm(ks[5], (DEPTH, HEAD_DIM), 0.1),
        "lambda_k1": nrm(ks[6], (DEPTH, HEAD_DIM), 0.1),
        "lambda_q2": nrm(ks[7], (DEPTH, HEAD_DIM), 0.1),
        "lambda_k2": nrm(ks[8], (DEPTH, HEAD_DIM), 0.1),
        "subln_g": 1.0 + nrm(ks[9], (DEPTH, V_DIM_B), 0.05),
        "w_branch_a": nrm(ks[10], (DEPTH, W_A, D_MODEL), W_A ** -0.5),
        "w_branch_b": nrm(ks[11], (DEPTH, W_B, D_MODEL), W_B ** -0.5),
        "w_out": nrm(ks[12], (DEPTH, D_MODEL, D_MODEL), D_MODEL ** -0.5 * DEEPNORM_BETA),
        "ln1_g": 1.0 + nrm(ks[13], (DEPTH, D_MODEL), 0.05),
        "ln1_b": nrm(ks[14], (DEPTH, D_MODEL), 0.02),
        "w_ff1": nrm(ks[15], (DEPTH, D_MODEL, D_FF), D_MODEL ** -0.5),
        "w_ff2": nrm(ks[16], (DEPTH, D_FF, D_MODEL), D_FF ** -0.5 * DEEPNORM_BETA),
        "ln2_g": 1.0 + nrm(ks[17], (DEPTH, D_MODEL), 0.05),
        "ln2_b": nrm(ks[18], (DEPTH, D_MODEL), 0.02),
    }


def reference(x, positions, w_in, b_gate, rel_bias, lambda_q1, lambda_k1, lambda_q2, lambda_k2,
              subln_g, w_branch_a, w_branch_b, w_out, ln1_g, ln1_b, w_ff1, w_ff2, ln2_g, ln2_b):
    b, s, _ = x.shape
    cos, sin = rope_tables(positions)
    splits = np.cumsum([W_A, W_A, W_A, QK_B, QK_B, W_B]).tolist()
    for l in range(DEPTH):
        u = x
        proj = jnp.einsum("bsd,dn->bsn", u, w_in[l])
        q_a, k_a, v_a, q_b, k_b, v_b, g = jnp.split(proj, splits, axis=-1)

        o_a = chunked_relbias_attention(
            q_a.reshape(b, s, H_A, HEAD_DIM), k_a.reshape(b, s, H_A, HEAD_DIM),
            v_a.reshape(b, s, H_A, HEAD_DIM), rel_bias[l])

        qb = partial_rope(q_b.reshape(b, s, 2 * H_B, HEAD_DIM), cos, sin).reshape(b, s, H_B, 2, HEAD_DIM)
        kb = partial_rope(k_b.reshape(b, s, 2 * H_B, HEAD_DIM), cos, sin).reshape(b, s, H_B, 2, HEAD_DIM)
        lam_init = 0.8 - 0.6 * math.exp(-0.3 * l)
        lam = (jnp.exp(jnp.sum(lambda_q1[l].astype(jnp.float32) * lambda_k1[l].astype(jnp.float32)))
               - jnp.exp(jnp.sum(lambda_q2[l].astype(jnp.float32) * lambda_k2[l].astype(jnp.float32)))
               + lam_init)
        o_b = differential_attention(qb, kb, v_b.reshape(b, s, H_B, V_DIM_B), lam, lam_init, subln_g[l])

        gates = jax.nn.sigmoid(g + b_gate[l])
        g_a, g_b = jnp.split(gates, 2, axis=-1)
        merged = (g_a * jnp.einsum("bsk,kd->bsd", o_a, w_branch_a[l])
                  + g_b * jnp.einsum("bsk,kd->bsd", o_b, w_branch_b[l]))
        mix = jnp.einsum("bsd,de->bse", merged, w_out[l])
        x = layer_norm(DEEPNORM_ALPHA * x + mix, ln1_g[l], ln1_b[l])

        h = jnp.square(jax.nn.relu(jnp.einsum("bsd,df->bsf", x, w_ff1[l])))
        ff = jnp.einsum("bsf,fd->bsd", h, w_ff2[l])
        x = layer_norm(DEEPNORM_ALPHA * x + ff, ln2_g[l], ln2_b[l])
    return x
```

```python
import math
import contextlib
import numpy as np
import ml_dtypes
import concourse.bass as bass
import concourse.mybir as mybir
from concourse.bass_utils import run_bass_kernel_spmd

F32 = mybir.dt.float32
BF16 = mybir.dt.bfloat16
I32 = mybir.dt.int32
AF = mybir.ActivationFunctionType
ALU = mybir.AluOpType
AX = mybir.AxisListType

D = 1024
S = 8192
NTOK = 2048
ALPHA = 2.0 ** 0.25
LAM_INIT = 0.8 - 0.6 * math.exp(0.0)
EPS = 1e-5
ENGINES = ("tensor", "vector", "scalar", "gpsimd", "sync")
DEBUG = False
SIMCHECK = False
STOP = 9
NBLK = 40


class StopBuild(Exception):
    pass


class Buf:
    __slots__ = ("name", "last_writer", "readers")

    def __init__(self, name):
        self.name = name
        self.last_writer = None
        self.readers = []


class DSem:
    __slots__ = ("name", "count", "handle", "cons", "bg")

    def __init__(self, name, handle, cons=False, bg=False):
        self.name = name
        self.count = 0
        self.handle = handle
        self.cons = cons
        self.bg = bg


class Op:
    __slots__ = ("eng", "fn", "seq", "waits", "dwaits", "signals", "dsem", "dcount", "clock", "dclock",
                 "is_dma", "sig")

    def __init__(self, eng, fn, is_dma=False, dsem=None):
        self.eng = eng
        self.fn = fn
        self.is_dma = is_dma
        self.dsem = dsem
        self.dcount = 0
        self.waits = []
        self.dwaits = []
        self.signals = False
        self.seq = 0
        self.sig = 0
        self.clock = None
        self.dclock = None


class Prog:
    def __init__(self, nc, stack):
        self.nc = nc
        self.stack = stack
        self.streams = {e: [] for e in ENGINES}
        self.nseq = {e: 0 for e in ENGINES}
        self.nsig = {e: 0 for e in ENGINES}
        self.last_compute = {e: None for e in ENGINES}
        self.known = {e: {f: 0 for f in ENGINES} for e in ENGINES}
        self.dknown = {e: {} for e in ENGINES}
        self.dsems = []
        self.esems = {e: stack.enter_context(nc.semaphore("s_" + e)) for e in ENGINES}
        self.bufs = {}

    def b(self, *key):
        bb = self.bufs.get(key)
        if bb is None:
            bb = Buf(key)
            self.bufs[key] = bb
        return bb

    def dsem(self, name, cons=False, bg=False):
        d = DSem(name, self.stack.enter_context(self.nc.semaphore("d_" + name)), cons, bg)
        self.dsems.append(d)
        return d

    def _add(self, op, reads, writes):
        eng = op.eng
        self.streams[eng].append(op)
        self.nseq[eng] += 1
        op.seq = self.nseq[eng]
        deps = []
        for bb in reads:
            if bb.last_writer is not None:
                deps.append(bb.last_writer)
            if bb.name[0] == "bank":
                for r_ in bb.readers:
                    if r_.eng != eng:
                        deps.append(r_)
        for bb in writes:
            if bb.last_writer is not None:
                deps.append(bb.last_writer)
            deps.extend(bb.readers)
        kn = self.known[eng]
        dk = self.dknown[eng]
        deps.sort(key=lambda d: -d.seq)
        for d in deps:
            if d is op:
                continue
            if d.is_dma:
                if dk.get(d.dsem, 0) >= d.dcount:
                    continue
                op.dwaits.append((d.dsem, d.dcount))
                dk[d.dsem] = d.dcount
            else:
                if d.eng == "tensor" and eng == "tensor" and not op.is_dma:
                    continue
                if kn[d.eng] >= d.seq:
                    continue
                op.waits.append(d)
                d.signals = True
                kn[d.eng] = d.seq
            if d.clock is not None:
                for f, s in d.clock.items():
                    if kn[f] < s:
                        kn[f] = s
                for ds, c in d.dclock.items():
                    if dk.get(ds, 0) < c:
                        dk[ds] = c
        op.clock = dict(kn)
        op.dclock = dict(dk)
        if not op.is_dma:
            op.clock[eng] = op.seq
            self.last_compute[eng] = op
        for bb in reads:
            bb.readers.append(op)
        for bb in writes:
            bb.last_writer = op
            bb.readers = []
        return op

    def op(self, eng, fn, reads=(), writes=()):
        return self._add(Op(eng, fn), reads, writes)

    def dma(self, eng, dsem, out, in_, reads=(), writes=()):
        def fn(e):
            return e.dma_start(out=out, in_=in_)
        op = Op(eng, fn, is_dma=True, dsem=dsem)
        dsem.count += 16
        op.dcount = dsem.count
        return self._add(op, reads, writes)

    def pe(self, fn, reads=(), writes=()):
        return self.op("tensor", fn, reads, writes)

    def dve(self, fn, reads=(), writes=()):
        return self.op("vector", fn, reads, writes)

    def act(self, fn, reads=(), writes=()):
        return self.op("scalar", fn, reads, writes)

    def pool(self, fn, reads=(), writes=()):
        return self.op("gpsimd", fn, reads, writes)

    def barrier(self):
        lasts = dict(self.last_compute)
        for e in ENGINES:
            op = Op(e, None)
            self.streams[e].append(op)
            self.nseq[e] += 1
            op.seq = self.nseq[e]
            kn = self.known[e]
            dk = self.dknown[e]
            for f, l in lasts.items():
                if l is not None and kn[f] < l.seq:
                    op.waits.append(l)
                    l.signals = True
                    kn[f] = l.seq
            for ds in self.dsems:
                if ds.bg:
                    continue
                if ds.count > dk.get(ds, 0):
                    op.dwaits.append((ds, ds.count))
                    dk[ds] = ds.count
            op.clock = dict(kn)
            op.dclock = dict(dk)
        for bb in self.bufs.values():
            lw = bb.last_writer
            if not (lw is not None and lw.is_dma and lw.dsem.bg):
                bb.last_writer = None
            bb.readers = []

    def _simcheck(self):
        if not hasattr(self, "simv"):
            self.simv = {e: 0 for e in ENGINES}
            self.simd = {}
        ptr = {e: 0 for e in ENGINES}
        n = {e: len(self.streams[e]) for e in ENGINES}
        print("simcheck: ops", n)
        progress = True
        while progress:
            progress = False
            for e in ENGINES:
                while ptr[e] < n[e]:
                    op = self.streams[e][ptr[e]]
                    ok = all(self.simv[d.eng] >= d.sig for d in op.waits) and \
                        all(self.simd.get(ds, 0) >= (ds.count if ds.cons else c) for ds, c in op.dwaits)
                    if not ok:
                        break
                    if op.fn is not None:
                        if op.is_dma:
                            self.simd[op.dsem] = self.simd.get(op.dsem, 0) + 16
                        elif op.signals:
                            self.simv[e] += 1
                    ptr[e] += 1
                    progress = True
        for e in ENGINES:
            if ptr[e] < n[e]:
                op = self.streams[e][ptr[e]]
                print("DEADLOCK", e, "at", ptr[e], "/", n[e], "waits", [(d.eng, d.sig, self.simv[d.eng]) for d in op.waits],
                      "dwaits", [(ds.name, c, self.simd.get(ds, 0)) for ds, c in op.dwaits])

    def flush(self, final_wait=()):
        for e in ENGINES:
            c = self.nsig[e]
            for op in self.streams[e]:
                if op.signals:
                    c += 1
                op.sig = c
            self.nsig[e] = c
        esems = self.esems
        streams = self.streams
        if SIMCHECK:
            self._simcheck()
        with self.nc.Block() as block:
            def run(ename, eng):
                for op in streams[ename]:
                    for d in op.waits:
                        eng.wait_ge(esems[d.eng], d.sig)
                    for ds, c in op.dwaits:
                        eng.wait_ge(ds.handle, ds.count if ds.cons else c)
                    if op.fn is None:
                        continue
                    ins = op.fn(eng)
                    if op.is_dma:
                        ins.then_inc(op.dsem.handle, 16)
                    elif op.signals:
                        ins.then_inc(esems[ename], 1)
                if ename == "sync":
                    for ds in final_wait:
                        eng.wait_ge(ds.handle, ds.count)

            @block.tensor
            def _(eng):
                run("tensor", eng)

            @block.vector
            def _(eng):
                run("vector", eng)

            @block.scalar
            def _(eng):
                run("scalar", eng)

            @block.gpsimd
            def _(eng):
                run("gpsimd", eng)

            @block.sync
            def _(eng):
                run("sync", eng)
        self.streams = {e: [] for e in ENGINES}


def build_nc():
    nc = bass.Bass("TRN2", target_bir_lowering=False)

    def din(name, shape, dt=F32):
        return nc.dram_tensor(name, shape, dt, kind="ExternalInput").ap()

    xT_all = din("xT_all", [D, S])
    xT_own = din("xT_own", [D, NTOK])
    xT_halo = din("xT_halo", [D, NTOK])
    pos_all = din("pos_all", [128, 64], I32)
    xT_qb = din("xT_qb", [D, NTOK])
    pos_qb = din("pos_qb", [128, 16], I32)
    invf_d = din("invf", [128, 8])
    ident_d = din("ident", [128, 128], BF16)
    w_in = din("w_in", [D, 5120])
    bgate_d = din("bgate", [128, 16])
    biasT_d = din("biasT", [128, 5 * 8 * 128])
    maskA_d = din("maskA", [128, 5 * 8 * 128])
    hv_d = din("hv", [128, 4])
    lamp_d = din("lamp", [1, 256])
    subg_d = din("subg", [128, 1])
    w_ba = din("w_ba", [512, D])
    w_bb = din("w_bb", [512, D])
    w_out = din("w_out", [D, D])
    ln1g_d = din("ln1g", [128, 8])
    ln1b_d = din("ln1b", [128, 8])
    ln2g_d = din("ln2g", [128, 8])
    ln2b_d = din("ln2b", [128, 8])
    w_ff1 = din("w_ff1", [D, 4096])
    w_ff2 = din("w_ff2", [4096, D])
    outT = nc.dram_tensor("outT", [D, NTOK], F32, kind="ExternalOutput").ap()
    if DEBUG:
        dbg_ob = nc.dram_tensor("dbg_ob", [128, 4 * NTOK], BF16, kind="ExternalOutput").ap()
        dbg_oa = nc.dram_tensor("dbg_oa", [64, 8 * 512], BF16, kind="ExternalOutput").ap()
        dbg_x1 = nc.dram_tensor("dbg_x1", [128, 8 * 512], F32, kind="ExternalOutput").ap()
    s_xown = nc.dram_tensor("s_xown", [D, NTOK], BF16).ap()
    s_xhalo = nc.dram_tensor("s_xhalo", [D, NTOK], BF16).ap()
    s_win = nc.dram_tensor("s_win", [D, 5120], BF16).ap()
    s_ba = nc.dram_tensor("s_ba", [512, D], BF16).ap()
    s_bb = nc.dram_tensor("s_bb", [512, D], BF16).ap()
    s_out = nc.dram_tensor("s_out", [D, D], BF16).ap()
    s_ff1 = nc.dram_tensor("s_ff1", [D, 4096], BF16).ap()
    s_ff2 = nc.dram_tensor("s_ff2", [4096, D], BF16).ap()

    with contextlib.ExitStack() as top:
        P = Prog(nc, top)
        b = P.b

        def sbt(stack, name, shape, dt):
            return stack.enter_context(nc.sbuf_tensor("sb_" + name, shape, dt))

        PS = [top.enter_context(nc.psum_tensor("ps%d" % i, [128, 1024], F32)) for i in range(4)]
        pring = [0]

        def bankbufs(i, half=None):
            if half is None:
                return [b("bank", 2 * i), b("bank", 2 * i + 1)]
            return [b("bank", 2 * i + half)]

        ident_b = sbt(top, "ident_b", [128, 128], BF16)
        ones_b = sbt(top, "ones_b", [128, 128], BF16)
        ones_f = sbt(top, "ones_f", [128, 128], F32)
        negm = sbt(top, "negm", [128, 128], BF16)
        eps_t = sbt(top, "eps_t", [128, 1], F32)
        neg_lam = sbt(top, "neg_lam", [1, 1], F32)
        gsc = sbt(top, "gsc", [128, 1], F32)
        bgate = sbt(top, "bgate", [128, 16], F32)
        ln1g = sbt(top, "ln1g", [128, 8], F32)
        ln1b = sbt(top, "ln1b", [128, 8], F32)
        ln2g = sbt(top, "ln2g", [128, 8], F32)
        ln2b = sbt(top, "ln2b", [128, 8], F32)
        hv = sbt(top, "hv", [128, 4], F32)
        o_bT = sbt(top, "o_bT", [128, 4, NTOK], BF16)

        d_out = P.dsem("out")
        swc = [0]

        def swdma(dst, src, bufs):
            swc[0] += 1
            P.dma("gpsimd", P.dsem("sw%d" % swc[0], bg=True), dst, src, writes=bufs)

        swdma(s_win[:, 1536:3072], w_in[:, 1536:3072], [b("scr_wkvq")])
        for cb in range(2):
            swdma(s_xown[:, cb * 1024:(cb + 1) * 1024], xT_own[:, cb * 1024:(cb + 1) * 1024], [b("scr_xown", cb)])
        for cb in range(2):
            swdma(s_xhalo[:, cb * 1024:(cb + 1) * 1024], xT_halo[:, cb * 1024:(cb + 1) * 1024], [b("scr_xhalo", cb)])
        for r0 in range(0, D, 256):
            swdma(s_win[r0:r0 + 256, 0:1536], w_in[r0:r0 + 256, 0:1536], [b("scr_winA", r0)])
        swdma(s_ba, w_ba, [b("scr_ba")])
        swdma(s_bb, w_bb, [b("scr_bb")])
        for r0 in range(0, D, 256):
            swdma(s_win[r0:r0 + 256, 3072:5120], w_in[r0:r0 + 256, 3072:5120], [b("scr_winG", r0)])
        for r0 in range(0, D, 512):
            swdma(s_out[r0:r0 + 512, :], w_out[r0:r0 + 512, :], [b("scr_out", r0)])
        for r0 in range(0, D, 128):
            swdma(s_ff1[r0:r0 + 128, :], w_ff1[r0:r0 + 128, :], [b("scr_ff1", r0)])
        for r0 in range(0, 4096, 512):
            swdma(s_ff2[r0:r0 + 512, :], w_ff2[r0:r0 + 512, :], [b("scr_ff2", r0)])
        R_WINA = [b("scr_winA", r0) for r0 in range(0, D, 256)]
        R_WING = [b("scr_winG", r0) for r0 in range(0, D, 256)]
        R_OUT = [b("scr_out", r0) for r0 in range(0, D, 512)]
        R_FF1 = [b("scr_ff1", r0) for r0 in range(0, D, 128)]

        ccount = [0]

        def cdma(queue, dst, src, bufs):
            ccount[0] += 1
            P.dma(queue, P.dsem("c%d" % ccount[0]), dst, src, writes=bufs)

        cdma("sync", ident_b[:], ident_d, [b("c_ident")])
        for i, (dst, src) in enumerate(((bgate, bgate_d), (ln1g, ln1g_d), (ln1b, ln1b_d), (ln2g, ln2g_d),
                                        (ln2b, ln2b_d), (hv, hv_d), (gsc, subg_d))):
            cdma("sync", dst[:], src, [b("c_small", i)])
        P.dve(lambda e: e.memset(ones_b[:], 1.0), writes=[b("c_ones_b")])
        P.dve(lambda e: e.memset(ones_f[:], 1.0), writes=[b("c_ones_f")])
        P.dve(lambda e: e.memset(negm[:, 0:64], 0.0), writes=[b("c_negm")])
        P.dve(lambda e: e.memset(negm[:, 64:128], -30000.0), writes=[b("c_negm")])
        P.dve(lambda e: e.memset(eps_t[:], EPS), writes=[b("c_eps")])
        P.dve(lambda e: e.tensor_scalar(gsc[:], gsc[:], 1.0 - LAM_INIT, None, ALU.mult),
              reads=[b("c_small", 6)], writes=[b("c_small", 6)])

        with contextlib.ExitStack() as sAB:
            K_bT = sbt(sAB, "K_bT", [128, 4, S], BF16)
            V_b = sbt(sAB, "V_b", [128, 64, 512], BF16)
            Q_bT = sbt(sAB, "Q_bT", [128, 4, NTOK], BF16)

            with contextlib.ExitStack() as sA:
                cs_all = [sbt(sA, "cos_all", [128, 64 * 8], F32), sbt(sA, "sin_all", [128, 64 * 8], F32)]
                cs_own = [sbt(sA, "cos_own", [128, 16 * 8], F32), sbt(sA, "sin_own", [128, 16 * 8], F32)]
                w_in_v = w_in.rearrange("(c p) n -> p c n", p=128)

                with contextlib.ExitStack() as s0:
                    lamp = sbt(s0, "lamp", [1, 256], F32)
                    lpr = sbt(s0, "lpr", [1, 128], F32)
                    ls = sbt(s0, "ls", [1, 2], F32)
                    posi = sbt(s0, "posi", [128, 80], I32)
                    posf = sbt(s0, "posf", [128, 80], F32)
                    invf = sbt(s0, "invf", [128, 8], F32)
                    ang = sbt(s0, "ang", [128, 640], F32)
                    a2 = sbt(s0, "a2", [128, 640], F32)
                    tf = sbt(s0, "tf", [128, 640], F32)
                    ti = sbt(s0, "ti", [128, 640], I32)
                    t0 = [b("p0")]
                    cdma("sync", lamp[:], lamp_d, [b("p0_lamp")])
                    cdma("sync", posi[:, 0:64], pos_all, [b("p0_pa")])
                    cdma("sync", posi[:, 64:80], pos_qb, [b("p0_po")])
                    cdma("sync", invf[:], invf_d, [b("p0_invf")])
                    P.dve(lambda e: e.memset(ls[:], 0.0), reads=[b("p0_lamp"), b("p0_pa"), b("p0_po"), b("p0_invf")], writes=t0)
                    P.dve(lambda e: e.tensor_tensor(lpr[:, 0:64], lamp[:, 0:64], lamp[:, 64:128], ALU.mult), t0, t0)
                    P.dve(lambda e: e.tensor_tensor(lpr[:, 64:128], lamp[:, 128:192], lamp[:, 192:256], ALU.mult), t0, t0)
                    P.dve(lambda e: e.reduce_sum(ls[:, 0:1], lpr[:, 0:64], AX.X), t0, t0)
                    P.dve(lambda e: e.reduce_sum(ls[:, 1:2], lpr[:, 64:128], AX.X), t0, t0)
                    P.act(lambda e: e.activation(ls[:], ls[:], AF.Exp), t0, t0)
                    P.dve(lambda e: e.tensor_tensor(neg_lam[:], ls[:, 1:2], ls[:, 0:1], ALU.subtract), t0, t0 + [b("consts")])
                    P.dve(lambda e: e.tensor_scalar(neg_lam[:], neg_lam[:], -LAM_INIT, None, ALU.add), t0, t0 + [b("consts")])
                    P.dve(lambda e: e.tensor_copy(posf[:], posi[:]), t0, t0)
                    P.dve(lambda e: e.tensor_tensor(
                        ang[:].rearrange("p (t f) -> p t f", f=8),
                        posf[:].unsqueeze(2).to_broadcast([128, 80, 8]),
                        invf[:].unsqueeze(1).to_broadcast([128, 80, 8]), ALU.mult), t0, t0)
                    C1 = 6.28125
                    C2 = 2.0 * math.pi - C1
                    for which, shift in ((0, math.pi / 2.0), (1, 0.0)):
                        P.dve(lambda e, shift=shift: e.tensor_scalar(a2[:], ang[:], shift, None, ALU.add), t0, t0)
                        P.dve(lambda e: e.tensor_scalar(tf[:], a2[:], 1.0 / (2.0 * math.pi), None, ALU.mult), t0, t0)
                        P.dve(lambda e: e.tensor_copy(ti[:], tf[:]), t0, t0)
                        P.dve(lambda e: e.tensor_copy(tf[:], ti[:]), t0, t0)
                        P.dve(lambda e: e.scalar_tensor_tensor(a2[:], tf[:], -C1, a2[:], ALU.mult, ALU.add), t0, t0)
                        P.dve(lambda e: e.scalar_tensor_tensor(a2[:], tf[:], -C2, a2[:], ALU.mult, ALU.add), t0, t0)
                        P.dve(lambda e: e.tensor_scalar(tf[:], a2[:], math.pi, -2.0 * math.pi, ALU.is_gt, ALU.mult), t0, t0)
                        P.dve(lambda e: e.tensor_tensor(a2[:], a2[:], tf[:], ALU.add), t0, t0)
                        P.dve(lambda e: e.tensor_scalar(tf[:], a2[:], -math.pi, 2.0 * math.pi, ALU.is_lt, ALU.mult), t0, t0)
                        P.dve(lambda e: e.tensor_tensor(a2[:], a2[:], tf[:], ALU.add), t0, t0)
                        P.act(lambda e, which=which: e.activation(cs_all[which][:], a2[:, 0:512], AF.Sin), t0, t0 + [b("cs")])
                        P.act(lambda e, which=which: e.activation(cs_own[which][:], a2[:, 512:640], AF.Sin), t0, t0 + [b("cs")])
                    P.barrier()
                    P.flush()
                    if STOP == 0:
                        return nc

                wkv = sbt(sA, "wkv", [128, 8, 1024], BF16)
                wqb = sbt(sA, "wqb", [128, 8, 512], BF16)
                xs = [sbt(sA, "xs%d" % i, [128, 8, 256], BF16) for i in range(2)]
                K_tm = [sbt(sA, "K_tm%d" % i, [128, 512], BF16) for i in range(2)]
                rt = [sbt(sA, "rt%d" % i, [128, 64], F32) for i in range(4)]
                d_xs = [P.dsem("xs%d" % i) for i in range(2)]
                d_w = [P.dsem("wA0"), P.dsem("wA1")]
                swin_v0 = s_win.rearrange("(c p) n -> p c n", p=128)
                P.dma("sync", d_w[0], wkv[:], swin_v0[:, :, 2048:3072], reads=[b("scr_wkvq")], writes=[b("wkv")])
                P.dma("sync", d_w[1], wqb[:], swin_v0[:, :, 1536:2048], reads=[b("scr_wkvq")], writes=[b("wqb")])

                def proj_rope_T(T, xsb, xbuf, tt, w, wbuf, ncol, cs, ci, dstT, v_dst):
                    pi = pring[0] % 2
                    pring[0] += 1
                    ps = PS[pi]
                    nh = ncol // 512
                    for half in range(nh):
                        for kc in range(8):
                            P.pe(lambda e, half=half, kc=kc: e.matmul(
                                ps[:, half * 512:(half + 1) * 512], xsb[:, kc, tt * 128:(tt + 1) * 128],
                                w[:, kc, half * 512:(half + 1) * 512], start=(kc == 0), stop=(kc == 7)),
                                reads=[xbuf, wbuf], writes=bankbufs(pi, half))
                    ktm = K_tm[T % 2]
                    kb = b("ktm", T % 2)
                    if v_dst is not None:
                        P.act(lambda e: e.copy(v_dst, ps[:, 512:1024]), reads=bankbufs(pi, 1), writes=[b("Vb", T)])
                    psv = ps[:, 0:512].rearrange("p (s d) -> p s d", d=64)
                    ktv = ktm[:].rearrange("p (s d) -> p s d", d=64)
                    P.dve(lambda e: e.tensor_copy(ktv[:, :, 16:64], psv[:, :, 16:64]), reads=bankbufs(pi, 0), writes=[kb])
                    cosb = cs[0][:, ci * 8:(ci + 1) * 8].unsqueeze(1).to_broadcast([128, 8, 8])
                    sinb = cs[1][:, ci * 8:(ci + 1) * 8].unsqueeze(1).to_broadcast([128, 8, 8])
                    r = [x[:].rearrange("p (s d) -> p s d", d=8) for x in rt]
                    rb = [b("rt")]
                    t1 = psv[:, :, 0:8]
                    t2 = psv[:, :, 8:16]
                    P.dve(lambda e: e.tensor_tensor(r[0], t1, cosb, ALU.mult), bankbufs(pi, 0) + [b("cs")], rb)
                    P.dve(lambda e: e.tensor_tensor(r[1], t2, sinb, ALU.mult), bankbufs(pi, 0) + [b("cs")], rb)
                    P.dve(lambda e: e.tensor_tensor(r[2], t1, sinb, ALU.mult), bankbufs(pi, 0) + [b("cs")], rb)
                    P.dve(lambda e: e.tensor_tensor(r[3], t2, cosb, ALU.mult), bankbufs(pi, 0) + [b("cs")], rb)
                    P.dve(lambda e: e.tensor_tensor(ktv[:, :, 0:8], r[0], r[1], ALU.subtract), rb, [kb])
                    P.dve(lambda e: e.tensor_tensor(ktv[:, :, 8:16], r[2], r[3], ALU.add), rb, [kb])
                    qi = 2 + (T % 2)
                    pt = PS[qi][:].bitcast(BF16)
                    for h in range(4):
                        P.pe(lambda e, h=h: e.transpose(pt[:, h * 128:(h + 1) * 128], ktm[:, h * 128:(h + 1) * 128],
                                                        ident_b[:]),
                             reads=[kb, b("consts")], writes=bankbufs(qi, 0))
                    P.act(lambda e: e.copy(dstT, pt[:, 0:512].rearrange("p (h t) -> p h t", h=4)),
                          reads=bankbufs(qi, 0), writes=[b("KT", T)])

                xall_v = xT_all.rearrange("(c p) t -> p c t", p=128)
                xqb_v = xT_qb.rearrange("(c p) t -> p c t", p=128)
                obf = o_bT[:].rearrange("p h t -> p (h t)")
                stg = [obf[:, i * 4096:(i + 1) * 4096].bitcast(F32).rearrange("p (c t) -> p c t", c=8) for i in range(2)]

                def load_xs(i):
                    if i < 32:
                        src = xall_v[:, :, i * 256:(i + 1) * 256]
                    elif i < 40:
                        src = xqb_v[:, :, (i - 32) * 256:(i - 31) * 256]
                    else:
                        return
                    P.dma("sync", d_xs[i % 2], stg[i % 2], src, writes=[b("stg", i % 2)])
                    P.act(lambda e, i=i: e.copy(xs[i % 2][:], stg[i % 2]), reads=[b("stg", i % 2)], writes=[b("xs", i % 2)])

                load_xs(0)
                load_xs(1)
                for blk in range(NBLK):
                    for tt in range(2):
                        T = blk * 2 + tt
                        if blk < 32:
                            proj_rope_T(T, xs[blk % 2], b("xs", blk % 2), tt, wkv, b("wkv"), 1024, cs_all, T,
                                        K_bT[:, :, T * 128:(T + 1) * 128], V_b[:, T, :])
                        else:
                            To = T - 64
                            proj_rope_T(T, xs[blk % 2], b("xs", blk % 2), tt, wqb, b("wqb"), 512, cs_own, To,
                                        Q_bT[:, :, To * 128:(To + 1) * 128], None)
                    if blk + 2 < NBLK:
                        load_xs(blk + 2)
                P.barrier()
                P.flush()
                if STOP == 1:
                    return nc

            with contextlib.ExitStack() as sB:
                Eb = [sbt(sB, "Eb%d" % i, [128, 1024], BF16) for i in range(3)]
                Osb = sbt(sB, "Osb", [128, 1024], F32)
                T1 = sbt(sB, "T1", [128, 512], F32)
                T2 = sbt(sB, "T2", [128, 512], F32)
                sqb = sbt(sB, "sqb", [128, 512], BF16)
                lnv = sbt(sB, "lnv", [128, 512], F32)
                zs = sbt(sB, "zs", [1, 1024], F32)
                ui = [0]
                ei = [0]

                for t in range(4):
                    low = (t < 2)
                    qs = slice(t * 512, (t + 1) * 512)
                    for h in range(4):
                        pend = {}

                        def v3q(ap2, kt):
                            return ap2[:, 8 * kt:512]

                        def emit_S(kt, h=h, qs=qs, low=low):
                            pi = ui[0] % 2
                            ui[0] += 1
                            ps = PS[pi]
                            ks = slice(kt * 128, (kt + 1) * 128)
                            for m in range(2):
                                pr = slice(m * 64, (m + 1) * 64)
                                P.pe(lambda e, m=m, pr=pr: e.matmul(
                                    v3q(ps[:, m * 512:(m + 1) * 512], kt), K_bT[pr, h, ks], v3q(Q_bT[pr, h, qs], kt),
                                    start=True, stop=not low), writes=bankbufs(pi, m))
                                if low:
                                    c0 = m * 512 + 8 * kt
                                    P.pe(lambda e, m=m, c0=c0: e.matmul(
                                        ps[:, c0:c0 + 8], negm[64 * m:64 * m + 1, :], ones_b[64 * m:64 * m + 1, 0:8],
                                        start=False, stop=True), writes=bankbufs(pi, m))
                            ek = ei[0] % 3
                            ei[0] += 1
                            E = Eb[ek]
                            Ev = E[:].rearrange("p (m c) -> p m c", m=2)[:, :, 8 * kt:512]
                            Pv = ps[:].rearrange("p (m c) -> p m c", m=2)[:, :, 8 * kt:512]
                            P.act(lambda e: e.activation(Ev, Pv, AF.Exp, scale=0.125),
                                  reads=bankbufs(pi), writes=[b("E", ek)])
                            pend[kt] = ek

                        def emit_PV(kt, h=h):
                            ek = pend.pop(kt)
                            E = Eb[ek]
                            first = (kt == 0)
                            last = (kt == 63)
                            vv = V_b[:, kt, h * 128:(h + 1) * 128]
                            for m in range(2):
                                rhs = v3q(E[:, m * 512:(m + 1) * 512], kt)
                                P.pe(lambda e, m=m, rhs=rhs: e.matmul(
                                    v3q(PS[2][:, m * 512:(m + 1) * 512], kt), vv, rhs, start=first, stop=last),
                                    reads=[b("E", ek)], writes=bankbufs(2, m))
                                P.pe(lambda e, m=m, rhs=rhs: e.matmul(
                                    v3q(PS[3][0:1, m * 512:(m + 1) * 512], kt), ones_b[:, 0:1], rhs, start=first, stop=last),
                                    reads=[b("E", ek)], writes=bankbufs(3, m))

                        emit_S(0)
                        emit_S(1)
                        for kt in range(64):
                            if kt + 2 < 64:
                                emit_S(kt + 2)
                            emit_PV(kt)
                        fb = [b("fin")]
                        P.act(lambda e: e.copy(zs[:], PS[3][0:1, :]), reads=bankbufs(3), writes=fb)
                        P.act(lambda e: e.copy(Osb[:], PS[2][:]), reads=bankbufs(2), writes=[b("Osb")])
                        P.dve(lambda e: e.reciprocal(zs[:], zs[:]), fb, fb)
                        P.dve(lambda e: e.tensor_scalar(zs[:, 512:1024], zs[:, 512:1024], neg_lam[0:1, 0:1], None,
                                                        ALU.mult), fb, fb)
                        pi = ui[0] % 2
                        ui[0] += 1
                        ps = PS[pi]
                        P.pe(lambda e, ps=ps: e.matmul(ps[:, 0:512], ones_f[0:1, :], zs[:, 0:512], start=True, stop=True),
                             reads=fb, writes=bankbufs(pi, 0))
                        P.pe(lambda e, ps=ps: e.matmul(ps[:, 512:1024], ones_f[0:1, :], zs[:, 512:1024], start=True,
                                                       stop=True), reads=fb, writes=bankbufs(pi, 1))
                        P.dve(lambda e, ps=ps: e.tensor_tensor(T1[:], ps[:, 0:512], Osb[:, 0:512], ALU.mult),
                              reads=bankbufs(pi, 0) + [b("Osb")], writes=[b("T1")])
                        P.dve(lambda e, ps=ps: e.tensor_tensor(T2[:], ps[:, 512:1024], Osb[:, 512:1024], ALU.mult),
                              reads=bankbufs(pi, 1) + [b("Osb")], writes=[b("T2")])
                        P.dve(lambda e: e.tensor_tensor(T1[:], T1[:], T2[:], ALU.add),
                              reads=[b("T1"), b("T2")], writes=[b("T1")])
                        P.act(lambda e: e.activation(sqb[:], T1[:], AF.Square), reads=[b("T1")], writes=[b("sqb")])
                        pj = ui[0] % 2
                        ui[0] += 1
                        ps2 = PS[pj]
                        P.pe(lambda e, ps2=ps2: e.matmul(ps2[:, 0:512], ones_b[:], sqb[:], start=True, stop=True),
                             reads=[b("sqb")], writes=bankbufs(pj, 0))
                        P.act(lambda e, ps2=ps2: e.activation(lnv[:], ps2[:, 0:512], AF.Ln, bias=eps_t[:], scale=1.0 / 128.0),
                              reads=bankbufs(pj, 0), writes=[b("lnv")])
                        P.act(lambda e: e.activation(lnv[:], lnv[:], AF.Exp, scale=-0.5), reads=[b("lnv")], writes=[b("lnv")])
                        P.dve(lambda e, h=h, qs=qs: e.scalar_tensor_tensor(
                            o_bT[:, h, qs].rearrange("p (a c) -> p c a", a=8),
                            T1[:].rearrange("p (c a) -> p c a", a=8), gsc[:, 0:1],
                            lnv[:].rearrange("p (c a) -> p c a", a=8), ALU.mult, ALU.mult),
                              reads=[b("T1"), b("lnv")], writes=[b("obT", t)])
                if DEBUG:
                    P.dma("sync", d_out, dbg_ob, o_bT[:].rearrange("p h t -> p (h t)"),
                          reads=[b("obT", t) for t in range(4)])
                P.barrier()
                P.flush()
                if STOP == 2:
                    return nc

        with contextlib.ExitStack() as sC:
            expB = sbt(sC, "expB", [128, 5, 1024], BF16)
            with contextlib.ExitStack() as sE:
                bf_ = sbt(sE, "bias_f", [128, 5120], F32)
                mf_ = sbt(sE, "mask_f", [128, 5120], F32)
                cdma("sync", bf_[:], biasT_d, [b("bf")])
                cdma("sync", mf_[:], maskA_d, [b("mf")])
                P.act(lambda e: e.activation(bf_[:], bf_[:], AF.Exp), reads=[b("bf")], writes=[b("bf")])
                P.dve(lambda e: e.tensor_tensor(expB[:].rearrange("p r n -> p (r n)"), bf_[:], mf_[:], ALU.mult),
                      reads=[b("bf"), b("mf")], writes=[b("expB")])
                P.barrier()
                P.flush()

            ring = [sbt(sC, "wr%d" % i, [128, 8192], BF16) for i in range(4)]
            d_ring = [P.dsem("wr%d" % i) for i in range(4)]
            xo_b = sbt(sC, "xo_b", [128, 8, 512], BF16)
            xhm = sbt(sC, "xhm", [128, 8, 512], BF16)
            xh_b = xhm
            mrg = xhm
            y1 = sbt(sC, "y1", [128, 8, 512], F32)
            Q_aT = sbt(sC, "Q_aT", [128, 4, 512], BF16)
            K_aT = sbt(sC, "K_aT", [128, 4, 1024], BF16)
            V_a = sbt(sC, "V_a", [128, 8, 8, 65], BF16)
            sig = [sbt(sC, "sig%d" % i, [128, 512], F32) for i in range(4)]
            EA4 = sbt(sC, "EA4", [128, 1024], BF16)
            EA = [sig[i][:].bitcast(BF16) for i in range(4)] + [EA4[:]]
            eab = [b("sig", i) for i in range(4)] + [b("EA4")]
            o_aT = sbt(sC, "o_aT", [64, 8, 512], BF16)
            zsa = sbt(sC, "zsa", [65, 1024], F32)
            OAs = sbt(sC, "OAs", [64, 1024], F32)
            x1b = sbt(sC, "x1b", [128, 8, 512], BF16)
            hT = sbt(sC, "hT", [128, 16, 512], BF16)
            ysq = hT
            st = [sbt(sC, "st%d" % i, [128, 512], F32) for i in range(3)]
            d_x = [P.dsem("xC%d" % i) for i in range(3)]
            P.dve(lambda e: e.memset(V_a[:, :, :, 64:65], 1.0), writes=[b("Va_ones")])

            swin_v = s_win.rearrange("(c p) n -> p c n", p=128)
            units = {}
            uorder = []
            ustate = {"nxt": 0}
            released = set()

            def unit(key, parts, rbufs):
                uorder.append((key, parts, rbufs))

            def pump():
                while ustate["nxt"] < len(uorder):
                    i = ustate["nxt"]
                    if i >= 4 and uorder[i - 4][0] not in released:
                        break
                    key, parts, rbufs = uorder[i]
                    ri = i % 4
                    for dst_fn, src in parts:
                        P.dma("sync", d_ring[ri], dst_fn(ring[ri]), src, reads=rbufs, writes=[b("ring", ri)])
                    units[key] = ri
                    ustate["nxt"] = i + 1

            def use(key):
                assert key in units, key
                return units[key]

            def release(key):
                released.add(key)
                pump()

            def v3(n):
                return lambda r: r[:, 0:8 * n].rearrange("p (c n) -> p c n", c=8)

            def gsel(gi):
                return lambda r: r[:, 0:8192].rearrange("p (c g n) -> p c g n", c=8, g=2)[:, :, gi, :]

            for t in range(4):
                unit((t, "qk"), [(v3(1024), swin_v[:, :, 0:1024])], R_WINA)
                unit((t, "v"), [(v3(512), swin_v[:, :, 1024:1536])], R_WINA)
                unit((t, "ba"), [(lambda r: r[0:64, 0:8192].rearrange("p (h n) -> p h n", h=8),
                                  s_ba.rearrange("(h p) n -> p h n", p=64))], [b("scr_ba")])
                unit((t, "bb"), [(lambda r: r[:, 0:4096].rearrange("p (h n) -> p h n", h=4),
                                  s_bb.rearrange("(h p) n -> p h n", p=128))], [b("scr_bb")])
                for mm in range(2):
                    unit((t, "gab", mm), [
                        (gsel(0), swin_v[:, :, 3072 + 512 * mm:3072 + 512 * (mm + 1)]),
                        (gsel(1), swin_v[:, :, 4096 + 512 * mm:4096 + 512 * (mm + 1)])], R_WING)
                unit((t, "out"), [(v3(1024), s_out.rearrange("(c p) n -> p c n", p=128))], R_OUT)
                for half in range(2):
                    for g in (2 * half, 2 * half + 1):
                        unit((t, "f1", g), [(v3(1024), s_ff1.rearrange("(c p) n -> p c n", p=128)[:, :, g * 1024:(g + 1) * 1024])], R_FF1)
                    for g in (2 * half, 2 * half + 1):
                        unit((t, "f2", g), [(v3(1024), s_ff2[g * 1024:(g + 1) * 1024, :].rearrange("(c p) n -> p c n", p=128))],
                             [b("scr_ff2", (g * 1024) // 512 * 512), b("scr_ff2", (g * 1024) // 512 * 512 + 512)])

            bk = [0]

            def next_bank():
                i = bk[0] % 8
                bk[0] += 1
                return PS[i // 2][:, (i % 2) * 512:(i % 2 + 1) * 512], [b("bank", i)]

            def next_pair():
                if bk[0] % 2:
                    bk[0] += 1
                i = (bk[0] % 8) // 2
                bk[0] += 2
                return PS[i], bankbufs(i)

            def layer_norm(g_t, b_t, want_bf16):
                yb = [b("y1", m) for m in range(8)]
                for m in range(8):
                    P.pool(lambda e, m=m: e.tensor_copy(x1b[:, m, :], y1[:, m, :]), reads=[yb[m]], writes=[b("x1b", m)])
                    P.act(lambda e, m=m: e.activation(ysq[:, m, :], y1[:, m, :], AF.Square), reads=[yb[m]],
                          writes=[b("hT", m)])
                ps_s, bs_s = next_bank()
                ps_q, bs_q = next_bank()
                for m in range(8):
                    P.pe(lambda e, m=m: e.matmul(ps_s, ones_b[:], x1b[:, m, :], start=(m == 0), stop=(m == 7)),
                         reads=[b("x1b", m)], writes=bs_s)
                for m in range(8):
                    P.pe(lambda e, m=m: e.matmul(ps_q, ones_b[:], ysq[:, m, :], start=(m == 0), stop=(m == 7)),
                         reads=[b("hT", m)], writes=bs_q)
                sb_ = [b("st")]
                P.act(lambda e: e.activation(st[0][:], ps_s, AF.Square, scale=1.0 / D), reads=bs_s, writes=sb_)
                P.dve(lambda e: e.scalar_tensor_tensor(st[1][:], ps_q, 1.0 / D, st[0][:], ALU.mult, ALU.subtract),
                      reads=bs_q + sb_, writes=sb_)
                P.act(lambda e: e.activation(st[1][:], st[1][:], AF.Ln, bias=eps_t[:], scale=1.0), sb_, sb_)
                P.act(lambda e: e.activation(st[1][:], st[1][:], AF.Exp, scale=-0.5), sb_, sb_)
                P.dve(lambda e: e.scalar_tensor_tensor(st[2][:], ps_s, -1.0 / D, st[1][:], ALU.mult, ALU.mult),
                      reads=bs_s + sb_, writes=sb_)
                for m in range(8):
                    P.pool(lambda e, m=m: e.tensor_tensor(y1[:, m, :], y1[:, m, :], st[1][:], ALU.mult),
                           reads=[yb[m]] + sb_, writes=[yb[m]])
                    P.pool(lambda e, m=m: e.tensor_tensor(y1[:, m, :], y1[:, m, :], st[2][:], ALU.add),
                           reads=[yb[m]] + sb_, writes=[yb[m]])
                    P.dve(lambda e, m=m: e.tensor_scalar(y1[:, m, :], y1[:, m, :], g_t[:, m:m + 1], b_t[:, m:m + 1],
                                                         ALU.mult, ALU.add), reads=[yb[m], b("consts")], writes=[yb[m]])
                    if want_bf16:
                        P.act(lambda e, m=m: e.copy(x1b[:, m, :], y1[:, m, :]), reads=[yb[m]], writes=[b("x1b", m)])

            xhalo_v = s_xhalo.rearrange("(c p) t -> p c t", p=128)
            xown_v = xT_own.rearrange("(c p) t -> p c t", p=128)
            xownb_v = s_xown.rearrange("(c p) t -> p c t", p=128)
            outT_v = outT.rearrange("(c p) t -> p c t", p=128)
            y1b = [b("y1", m) for m in range(8)]
            XHM = b("xhm")

            def do_slot(t):
                ts_ = slice(t * 512, (t + 1) * 512)
                P.dma("scalar", d_x[0], xo_b[:], xownb_v[:, :, ts_], reads=[b("scr_xown", t // 2)], writes=[b("xo")])
                P.dma("scalar", d_x[1], xh_b[:], xhalo_v[:, :, ts_], reads=[b("scr_xhalo", t // 2)], writes=[XHM])
                P.dma("scalar", d_x[2], y1[:], xown_v[:, :, ts_], writes=y1b)
                Wqk = v3(1024)(ring[use((t, "qk"))])
                wqk_b = [b("ring", use((t, "qk")))]
                Wv = v3(512)(ring[use((t, "v"))])
                wv_b = [b("ring", use((t, "v")))]
                for which, xsrc, xb_, dst, dbuf in (
                        (0, xo_b, b("xo"), lambda hp: Q_aT[:, hp, :], "QaT"),
                        (1, xh_b, XHM, lambda hp: K_aT[:, hp, 0:512], "KaT0"),
                        (1, xo_b, b("xo"), lambda hp: K_aT[:, hp, 512:1024], "KaT1")):
                    for hp in range(4):
                        ps, bs = next_bank()
                        c0 = which * 512 + hp * 128
                        for kc in range(8):
                            P.pe(lambda e, ps=ps, kc=kc, c0=c0, xsrc=xsrc: e.matmul(
                                ps, Wqk[:, kc, c0:c0 + 128], xsrc[:, kc, :], start=(kc == 0), stop=(kc == 7)),
                                reads=wqk_b + [xb_], writes=bs)
                        if hp % 2 == 0:
                            P.act(lambda e, ps=ps, hp=hp, dst=dst: e.copy(dst(hp), ps), reads=bs, writes=[b(dbuf, hp)])
                        else:
                            P.dve(lambda e, ps=ps, hp=hp, dst=dst: e.tensor_copy(dst(hp), ps), reads=bs,
                                  writes=[b(dbuf, hp)])
                release((t, "qk"))
                for si, (xsrc, xb_) in enumerate(((xh_b, XHM), (xo_b, b("xo")))):
                    for tt in range(4):
                        kt = si * 4 + tt
                        ps, bs = next_bank()
                        for kc in range(8):
                            P.pe(lambda e, ps=ps, kc=kc, tt=tt, xsrc=xsrc: e.matmul(
                                ps, xsrc[:, kc, tt * 128:(tt + 1) * 128], Wv[:, kc, :], start=(kc == 0), stop=(kc == 7)),
                                reads=wv_b + [xb_], writes=bs)
                        dstv = V_a[:, kt, :, 0:64]
                        srcv = ps.rearrange("p (h d) -> p h d", h=8)
                        if kt % 2 == 0:
                            P.act(lambda e, dstv=dstv, srcv=srcv: e.copy(dstv, srcv), reads=bs, writes=[b("Va", kt)])
                        else:
                            P.dve(lambda e, dstv=dstv, srcv=srcv: e.tensor_copy(dstv, srcv), reads=bs, writes=[b("Va", kt)])
                release((t, "v"))
                if STOP == 3:
                    raise StopBuild()
                P.dve(lambda e: e.tensor_scalar(
                    V_a[:, 0:4, :, 64:65], ones_b[:, 0:32].rearrange("p (a c o) -> p a c o", a=4, c=8),
                    hv[:, t:t + 1], None, ALU.mult),
                    reads=[b("consts")], writes=[b("Va_ones")])
                if STOP == 40:
                    raise StopBuild()
                for g in range(4):
                    gq = slice(g * 128, (g + 1) * 128)
                    for r in range(5):
                        kt = g + r
                        ps, bs = next_pair()
                        for h in range(8):
                            hp, e2 = divmod(h, 2)
                            pr = slice(e2 * 64, (e2 + 1) * 64)
                            hc = e2 * 512 + hp * 128
                            P.pe(lambda e, ps=ps, hc=hc, hp=hp, pr=pr, kt=kt, gq=gq: e.matmul(
                                ps[:, hc:hc + 128], K_aT[pr, hp, kt * 128:(kt + 1) * 128], Q_aT[pr, hp, gq],
                                start=True, stop=True),
                                reads=[b("QaT", hp), b("KaT0", hp), b("KaT1", hp)], writes=bs)
                        P.act(lambda e, ps=ps, r=r: e.activation(EA[r], ps[:], AF.Exp, scale=0.125),
                              reads=bs, writes=[eab[r]])
                        P.dve(lambda e, r=r: e.tensor_tensor(EA[r], EA[r], expB[:, r, :], ALU.mult),
                              reads=[eab[r], b("expB")], writes=[eab[r]])
                    if STOP == 41:
                        raise StopBuild()
                    po, bo = next_pair()
                    for h in range(8):
                        for r in range(5):
                            kt = g + r
                            hc = (h % 2) * 512 + (h // 2) * 128
                            P.pe(lambda e, h=h, r=r, kt=kt, po=po, hc=hc: e.matmul(
                                po[0:65, h * 128:(h + 1) * 128], V_a[:, kt, h, :], EA[r][:, hc:hc + 128],
                                start=(r == 0), stop=(r == 4)),
                                reads=[eab[r], b("Va", kt), b("Va_ones")], writes=bo)
                    if STOP == 42:
                        raise StopBuild()
                    zb = [b("zsa")]
                    P.act(lambda e, po=po: e.copy(zsa[64:65, :], po[64:65, :]), reads=bo, writes=zb)
                    P.act(lambda e, po=po: e.copy(OAs[:], po[0:64, :]), reads=bo, writes=[b("OAs")])
                    P.dve(lambda e: e.reciprocal(zsa[64:65, :], zsa[64:65, :]), zb, zb)
                    pz, bz = next_pair()
                    for hf in range(2):
                        P.pe(lambda e, hf=hf, pz=pz: e.matmul(
                            pz[0:64, hf * 512:(hf + 1) * 512], ones_f[64:65, 0:64], zsa[64:65, hf * 512:(hf + 1) * 512],
                            start=True, stop=True), reads=zb, writes=bz)
                    P.dve(lambda e, pz=pz, gq=gq: e.tensor_tensor(
                        o_aT[:, :, gq], OAs[:].rearrange("p (h q) -> p h q", h=8),
                        pz[0:64, :].rearrange("p (h q) -> p h q", h=8), ALU.mult),
                        reads=bz + [b("OAs")], writes=[b("oaT")])
                    if STOP == 43:
                        raise StopBuild()
                if DEBUG and t == 0:
                    P.dma("sync", d_out, dbg_oa, o_aT[:].rearrange("p h t -> p (h t)"), reads=[b("oaT")])
                if STOP == 4:
                    raise StopBuild()
                Wba = ring[use((t, "ba"))][0:64, 0:8192].rearrange("p (h n) -> p h n", h=8)
                wba_b = [b("ring", use((t, "ba")))]
                Wbb = ring[use((t, "bb"))][:, 0:4096].rearrange("p (h n) -> p h n", h=4)
                wbb_b = [b("ring", use((t, "bb")))]
                for m in range(8):
                    mm, ml = divmod(m, 4)
                    Wg = ring[use((t, "gab", mm))][:, 0:8192].rearrange("p (c g n) -> p c g n", c=8, g=2)
                    wg_b = [b("ring", use((t, "gab", mm)))]
                    pga, bga = next_bank()
                    pgb, bgb = next_bank()
                    for gi, (pg, bgx) in enumerate(((pga, bga), (pgb, bgb))):
                        for kc in range(8):
                            P.pe(lambda e, pg=pg, kc=kc, gi=gi, Wg=Wg, ml=ml: e.matmul(
                                pg, Wg[:, kc, gi, ml * 128:(ml + 1) * 128], xo_b[:, kc, :], start=(kc == 0), stop=(kc == 7)),
                                reads=wg_b + [b("xo")], writes=bgx)
                    sa = sig[(2 * m) % 4]
                    sb2 = sig[(2 * m + 1) % 4]
                    sab = [b("sig", (2 * m) % 4)]
                    sbb = [b("sig", (2 * m + 1) % 4)]
                    P.act(lambda e, pga=pga, sa=sa, m=m: e.activation(sa[:], pga, AF.Sigmoid, bias=bgate[:, m:m + 1]),
                          reads=bga + [b("consts")], writes=sab)
                    P.act(lambda e, pgb=pgb, sb2=sb2, m=m: e.activation(sb2[:], pgb, AF.Sigmoid, bias=bgate[:, 8 + m:9 + m]),
                          reads=bgb + [b("consts")], writes=sbb)
                    pba, bba = next_bank()
                    pbb, bbb = next_bank()
                    for h in range(8):
                        P.pe(lambda e, h=h, pba=pba, m=m: e.matmul(
                            pba, Wba[:, h, m * 128:(m + 1) * 128], o_aT[:, h, :], start=(h == 0), stop=(h == 7)),
                            reads=wba_b + [b("oaT")], writes=bba)
                    for hb in range(4):
                        P.pe(lambda e, hb=hb, pbb=pbb, m=m: e.matmul(
                            pbb, Wbb[:, hb, m * 128:(m + 1) * 128], o_bT[:, hb, ts_], start=(hb == 0), stop=(hb == 3)),
                            reads=wbb_b, writes=bbb)
                    P.dve(lambda e, sa=sa, pba=pba: e.tensor_tensor(sa[:], sa[:], pba, ALU.mult), reads=sab + bba, writes=sab)
                    P.dve(lambda e, sb2=sb2, pbb=pbb: e.tensor_tensor(sb2[:], sb2[:], pbb, ALU.mult), reads=sbb + bbb, writes=sbb)
                    P.pool(lambda e, sa=sa, sb2=sb2, m=m: e.tensor_tensor(mrg[:, m, :], sa[:], sb2[:], ALU.add),
                           reads=sab + sbb, writes=[XHM])
                    if ml == 3:
                        release((t, "gab", mm))
                release((t, "ba"))
                release((t, "bb"))
                if STOP == 5:
                    raise StopBuild()
                Wo = v3(1024)(ring[use((t, "out"))])
                wo_b = [b("ring", use((t, "out")))]
                for m in range(8):
                    ps, bs = next_bank()
                    for kc in range(8):
                        P.pe(lambda e, ps=ps, kc=kc, m=m: e.matmul(
                            ps, Wo[:, kc, m * 128:(m + 1) * 128], mrg[:, kc, :], start=(kc == 0), stop=(kc == 7)),
                            reads=wo_b + [XHM], writes=bs)
                    P.dve(lambda e, ps=ps, m=m: e.scalar_tensor_tensor(y1[:, m, :], y1[:, m, :], ALPHA, ps, ALU.mult, ALU.add),
                          reads=bs + [b("y1", m)], writes=[b("y1", m)])
                release((t, "out"))
                layer_norm(ln1g, ln1b, True)
                if DEBUG and t == 0:
                    P.dma("sync", d_out, dbg_x1, y1[:].rearrange("p c t -> p (c t)"), reads=y1b)
                if STOP == 6:
                    raise StopBuild()
                for half in range(2):
                    for fc in range(16):
                        f = half * 16 + fc
                        grp, wi = divmod(f, 8)
                        W1 = v3(1024)(ring[use((t, "f1", grp))])
                        w1_b = [b("ring", use((t, "f1", grp)))]
                        ps, bs = next_bank()
                        for kc in range(8):
                            P.pe(lambda e, ps=ps, kc=kc, wi=wi, W1=W1: e.matmul(
                                ps, W1[:, kc, wi * 128:(wi + 1) * 128], x1b[:, kc, :], start=(kc == 0), stop=(kc == 7)),
                                reads=w1_b + [b("x1b", kc)], writes=bs)
                        rl = sig[fc % 4]
                        rlb = [b("sig", fc % 4)]
                        P.act(lambda e, ps=ps, rl=rl: e.activation(rl[:], ps, AF.Relu), reads=bs, writes=rlb)
                        P.dve(lambda e, ps=ps, rl=rl, fc=fc: e.tensor_tensor(hT[:, fc, :], ps, rl[:], ALU.mult),
                              reads=bs + rlb, writes=[b("hT", fc)])
                        if wi == 7:
                            release((t, "f1", grp))
                    for m in range(8):
                        ps, bs = next_bank()
                        for fc in range(16):
                            f = half * 16 + fc
                            grp, wi = divmod(f, 8)
                            W2 = v3(1024)(ring[use((t, "f2", grp))])
                            w2_b = [b("ring", use((t, "f2", grp)))]
                            P.pe(lambda e, ps=ps, fc=fc, wi=wi, W2=W2, m=m: e.matmul(
                                ps, W2[:, wi, m * 128:(m + 1) * 128], hT[:, fc, :], start=(fc == 0), stop=(fc == 15)),
                                reads=w2_b + [b("hT", fc)], writes=bs)
                        if half == 0:
                            P.dve(lambda e, ps=ps, m=m: e.scalar_tensor_tensor(y1[:, m, :], y1[:, m, :], ALPHA, ps,
                                                                               ALU.mult, ALU.add),
                                  reads=bs + [b("y1", m)], writes=[b("y1", m)])
                        else:
                            P.dve(lambda e, ps=ps, m=m: e.tensor_tensor(y1[:, m, :], y1[:, m, :], ps, ALU.add),
                                  reads=bs + [b("y1", m)], writes=[b("y1", m)])
                    release((t, "f2", 2 * half))
                    release((t, "f2", 2 * half + 1))
                layer_norm(ln2g, ln2b, False)
                P.dma("sync", d_out, outT_v[:, :, ts_], y1[:], reads=y1b)

            pump()
            try:
                for t in range(4):
                    do_slot(t)
                    if STOP == 7:
                        raise StopBuild()
            except StopBuild:
                pass
            P.barrier()
            P.flush(final_wait=[d_out])

    return nc


def _own_blocks(j):
    return [j, 7 - j, 8 + j, 15 - j]


def _consts():
    p = np.arange(128)[:, None, None]
    r = np.arange(5)[None, :, None]
    q = np.arange(128)[None, None, :]
    kk = r * 128 + p
    dist = (512 + q) - kk
    idx = np.clip(dist, -128, 128) + 128
    kc = kk // 64
    qc = 8 + q // 64
    valid = (kc <= qc) & (kc >= qc - 8)
    return idx, valid


def prep_inputs(x, positions, w_in, b_gate, rel_bias, lambda_q1, lambda_k1, lambda_q2, lambda_k2, subln_g,
                w_branch_a, w_branch_b, w_out, ln1_g, ln1_b, w_ff1, w_ff2, ln2_g, ln2_b):
    x = np.asarray(x, dtype=np.float32)
    positions = np.asarray(positions)
    f32 = lambda a: np.ascontiguousarray(np.asarray(a, dtype=np.float32))
    idx, valid = _consts()
    rb = np.asarray(rel_bias, dtype=np.float32)[0]
    biasT = rb[:, idx]
    biasT = biasT[[0, 2, 4, 6, 1, 3, 5, 7]]
    biasT = np.ascontiguousarray(biasT.transpose(1, 2, 0, 3)).reshape(128, 5 * 8 * 128)
    maskA = np.broadcast_to(valid[:, :, None, :], (128, 5, 8, 128)).astype(np.float32).reshape(128, 5 * 8 * 128)
    invf = (np.float32(500000.0) ** (-np.arange(0, 16, 2, dtype=np.float32) / np.float32(16))).astype(np.float32)
    invf = np.ascontiguousarray(np.broadcast_to(invf[None, :], (128, 8)))
    ident = np.eye(128, dtype=np.float32).astype(ml_dtypes.bfloat16)
    pl = lambda v, c: np.ascontiguousarray(np.asarray(v, np.float32).reshape(c, 128).T)
    lamp = np.concatenate([np.asarray(a, np.float32).reshape(-1) for a in
                           (lambda_q1, lambda_k1, lambda_q2, lambda_k2)])[None, :]
    shared = {
        "invf": invf, "ident": ident, "w_in": f32(w_in[0]), "bgate": pl(b_gate[0], 16),
        "biasT": f32(biasT), "maskA": f32(maskA), "lamp": f32(lamp), "subg": f32(np.asarray(subln_g)[0].reshape(128, 1)),
        "w_ba": f32(w_branch_a[0]), "w_bb": f32(w_branch_b[0]), "w_out": f32(w_out[0]),
        "ln1g": pl(ln1_g[0], 8), "ln1b": pl(ln1_b[0], 8), "ln2g": pl(ln2_g[0], 8), "ln2b": pl(ln2_b[0], 8),
        "w_ff1": f32(w_ff1[0]), "w_ff2": f32(w_ff2[0]),
    }
    xT = [np.ascontiguousarray(x[bb].T) for bb in range(2)]
    in_maps = []
    own_tok = []
    for c in range(8):
        bi, j = divmod(c, 4)
        ob = _own_blocks(j)
        tok = np.concatenate([np.arange(o * 512, (o + 1) * 512) for o in ob])
        own_tok.append((bi, tok))
        xo = np.ascontiguousarray(xT[bi][:, tok])
        xh = np.zeros((D, NTOK), np.float32)
        hvv = np.ones((4,), np.float32)
        for s_, o in enumerate(ob):
            if o - 1 >= 0:
                xh[:, s_ * 512:(s_ + 1) * 512] = xT[bi][:, (o - 1) * 512:o * 512]
            else:
                hvv[s_] = 0.0
        qtok = np.empty((NTOK,), np.int64)
        qq = np.arange(8)[:, None]
        bb2 = np.arange(64)[None, :]
        for s_, o in enumerate(ob):
            qtok[s_ * 512:(s_ + 1) * 512] = (bb2 * 128 + 8 * o + qq).T.reshape(-1)
        m = dict(shared)
        m.update({
            "xT_all": xT[bi], "xT_own": xo, "xT_halo": xh,
            "pos_all": np.ascontiguousarray(positions[bi].astype(np.int32).reshape(64, 128).T),
            "xT_qb": np.ascontiguousarray(xT[bi][:, qtok]),
            "pos_qb": np.ascontiguousarray(positions[bi][qtok].astype(np.int32).reshape(16, 128).T),
            "hv": np.ascontiguousarray(np.broadcast_to(hvv[None, :], (128, 4))).astype(np.float32),
        })
        in_maps.append(m)
    return in_maps, own_tok


def kernel(**inputs):
    in_maps, own_tok = prep_inputs(**inputs)
    nc = build_nc()
    res = run_bass_kernel_spmd(nc, in_maps, core_ids=list(range(8)))
    out = np.empty((2, S, D), np.float32)
    for c in range(8):
        bi, tok = own_tok[c]
        out[bi, tok, :] = np.asarray(res.results[c]["outT"], dtype=np.float32).T
    if DEBUG:
        kernel.dbg = res.results
    return out
```

```python
import math
import contextlib
import numpy as np
import ml_dtypes
import concourse.bass as bass
import concourse.mybir as mybir
from concourse.bass_utils import run_bass_kernel_spmd

F32 = mybir.dt.float32
BF16 = mybir.dt.bfloat16
I32 = mybir.dt.int32
AF = mybir.ActivationFunctionType
ALU = mybir.AluOpType
AX = mybir.AxisListType

D = 1024
S = 8192
NTOK = 2048
ALPHA = 2.0 ** 0.25
LAM_INIT = 0.8 - 0.6 * math.exp(0.0)
EPS = 1e-5
ENGINES = ("tensor", "vector", "scalar", "gpsimd", "sync")
DEBUG = False
SIMCHECK = False
STOP = 9
NBLK = 40


class StopBuild(Exception):
    pass


class Buf:
    __slots__ = ("name", "last_writer", "readers")

    def __init__(self, name):
        self.name = name
        self.last_writer = None
        self.readers = []


class DSem:
    __slots__ = ("name", "count", "handle", "cons", "bg")

    def __init__(self, name, handle, cons=False, bg=False):
        self.name = name
        self.count = 0
        self.handle = handle
        self.cons = cons
        self.bg = bg


class Op:
    __slots__ = ("eng", "fn", "seq", "waits", "dwaits", "signals", "dsem", "dcount", "clock", "dclock",
                 "is_dma", "sig")

    def __init__(self, eng, fn, is_dma=False, dsem=None):
        self.eng = eng
        self.fn = fn
        self.is_dma = is_dma
        self.dsem = dsem
        self.dcount = 0
        self.waits = []
        self.dwaits = []
        self.signals = False
        self.seq = 0
        self.sig = 0
        self.clock = None
        self.dclock = None


class Prog:
    def __init__(self, nc, stack):
        self.nc = nc
        self.stack = stack
        self.streams = {e: [] for e in ENGINES}
        self.nseq = {e: 0 for e in ENGINES}
        self.nsig = {e: 0 for e in ENGINES}
        self.last_compute = {e: None for e in ENGINES}
        self.known = {e: {f: 0 for f in ENGINES} for e in ENGINES}
        self.dknown = {e: {} for e in ENGINES}
        self.dsems = []
        self.esems = {e: stack.enter_context(nc.semaphore("s_" + e)) for e in ENGINES}
        self.bufs = {}

    def b(self, *key):
        bb = self.bufs.get(key)
        if bb is None:
            bb = Buf(key)
            self.bufs[key] = bb
        return bb

    def dsem(self, name, cons=False, bg=False):
        d = DSem(name, self.stack.enter_context(self.nc.semaphore("d_" + name)), cons, bg)
        self.dsems.append(d)
        return d

    def _add(self, op, reads, writes):
        eng = op.eng
        self.streams[eng].append(op)
        self.nseq[eng] += 1
        op.seq = self.nseq[eng]
        deps = []
        for bb in reads:
            if bb.last_writer is not None:
                deps.append(bb.last_writer)
            if bb.name[0] == "bank":
                for r_ in bb.readers:
                    if r_.eng != eng:
                        deps.append(r_)
        for bb in writes:
            if bb.last_writer is not None:
                deps.append(bb.last_writer)
            deps.extend(bb.readers)
        kn = self.known[eng]
        dk = self.dknown[eng]
        deps.sort(key=lambda d: -d.seq)
        for d in deps:
            if d is op:
                continue
            if d.is_dma:
                if dk.get(d.dsem, 0) >= d.dcount:
                    continue
                op.dwaits.append((d.dsem, d.dcount))
                dk[d.dsem] = d.dcount
            else:
                if d.eng == "tensor" and eng == "tensor" and not op.is_dma:
                    continue
                if kn[d.eng] >= d.seq:
                    continue
                op.waits.append(d)
                d.signals = True
                kn[d.eng] = d.seq
            if d.clock is not None:
                for f, s in d.clock.items():
                    if kn[f] < s:
                        kn[f] = s
                for ds, c in d.dclock.items():
                    if dk.get(ds, 0) < c:
                        dk[ds] = c
        op.clock = dict(kn)
        op.dclock = dict(dk)
        if not op.is_dma:
            op.clock[eng] = op.seq
            self.last_compute[eng] = op
        for bb in reads:
            bb.readers.append(op)
        for bb in writes:
            bb.last_writer = op
            bb.readers = []
        return op

    def op(self, eng, fn, reads=(), writes=()):
        return self._add(Op(eng, fn), reads, writes)

    def dma(self, eng, dsem, out, in_, reads=(), writes=()):
        def fn(e):
            return e.dma_start(out=out, in_=in_)
        op = Op(eng, fn, is_dma=True, dsem=dsem)
        dsem.count += 16
        op.dcount = dsem.count
        return self._add(op, reads, writes)

    def pe(self, fn, reads=(), writes=()):
        return self.op("tensor", fn, reads, writes)

    def dve(self, fn, reads=(), writes=()):
        return self.op("vector", fn, reads, writes)

    def act(self, fn, reads=(), writes=()):
        return self.op("scalar", fn, reads, writes)

    def pool(self, fn, reads=(), writes=()):
        return self.op("gpsimd", fn, reads, writes)

    def barrier(self):
        lasts = dict(self.last_compute)
        for e in ENGINES:
            op = Op(e, None)
            self.streams[e].append(op)
            self.nseq[e] += 1
            op.seq = self.nseq[e]
            kn = self.known[e]
            dk = self.dknown[e]
            for f, l in lasts.items():
                if l is not None and kn[f] < l.seq:
                    op.waits.append(l)
                    l.signals = True
                    kn[f] = l.seq
            for ds in self.dsems:
                if ds.bg:
                    continue
                if ds.count > dk.get(ds, 0):
                    op.dwaits.append((ds, ds.count))
                    dk[ds] = ds.count
            op.clock = dict(kn)
            op.dclock = dict(dk)
        for bb in self.bufs.values():
            lw = bb.last_writer
            if not (lw is not None and lw.is_dma and lw.dsem.bg):
                bb.last_writer = None
            bb.readers = []

    def _simcheck(self):
        if not hasattr(self, "simv"):
            self.simv = {e: 0 for e in ENGINES}
            self.simd = {}
        ptr = {e: 0 for e in ENGINES}
        n = {e: len(self.streams[e]) for e in ENGINES}
        print("simcheck: ops", n)
        progress = True
        while progress:
            progress = False
            for e in ENGINES:
                while ptr[e] < n[e]:
                    op = self.streams[e][ptr[e]]
                    ok = all(self.simv[d.eng] >= d.sig for d in op.waits) and \
                        all(self.simd.get(ds, 0) >= (ds.count if ds.cons else c) for ds, c in op.dwaits)
                    if not ok:
                        break
                    if op.fn is not None:
                        if op.is_dma:
                            self.simd[op.dsem] = self.simd.get(op.dsem, 0) + 16
                        elif op.signals:
                            self.simv[e] += 1
                    ptr[e] += 1
                    progress = True
        for e in ENGINES:
            if ptr[e] < n[e]:
                op = self.streams[e][ptr[e]]
                print("DEADLOCK", e, "at", ptr[e], "/", n[e], "waits", [(d.eng, d.sig, self.simv[d.eng]) for d in op.waits],
                      "dwaits", [(ds.name, c, self.simd.get(ds, 0)) for ds, c in op.dwaits])

    def flush(self, final_wait=()):
        for e in ENGINES:
            c = self.nsig[e]
            for op in self.streams[e]:
                if op.signals:
                    c += 1
                op.sig = c
            self.nsig[e] = c
        esems = self.esems
        streams = self.streams
        if SIMCHECK:
            self._simcheck()
        with self.nc.Block() as block:
            def run(ename, eng):
                for op in streams[ename]:
                    for d in op.waits:
                        eng.wait_ge(esems[d.eng], d.sig)
                    for ds, c in op.dwaits:
                        eng.wait_ge(ds.handle, ds.count if ds.cons else c)
                    if op.fn is None:
                        continue
                    ins = op.fn(eng)
                    if op.is_dma:
                        ins.then_inc(op.dsem.handle, 16)
                    elif op.signals:
                        ins.then_inc(esems[ename], 1)
                if ename == "sync":
                    for ds in final_wait:
                        eng.wait_ge(ds.handle, ds.count)

            @block.tensor
            def _(eng):
                run("tensor", eng)

            @block.vector
            def _(eng):
                run("vector", eng)

            @block.scalar
            def _(eng):
                run("scalar", eng)

            @block.gpsimd
            def _(eng):
                run("gpsimd", eng)

            @block.sync
            def _(eng):
                run("sync", eng)
        self.streams = {e: [] for e in ENGINES}


def build_nc():
    nc = bass.Bass("TRN2", target_bir_lowering=False)

    def din(name, shape, dt=F32):
        return nc.dram_tensor(name, shape, dt, kind="ExternalInput").ap()

    xT_all = din("xT_all", [D, S])
    xT_own = din("xT_own", [D, NTOK])
    xT_halo = din("xT_halo", [D, NTOK])
    pos_all = din("pos_all", [128, 64], I32)
    xT_qb = din("xT_qb", [D, NTOK])
    pos_qb = din("pos_qb", [128, 16], I32)
    invf_d = din("invf", [128, 8])
    ident_d = din("ident", [128, 128], BF16)
    w_in = din("w_in", [D, 5120])
    bgate_d = din("bgate", [128, 16])
    biasT_d = din("biasT", [128, 5 * 8 * 128])
    maskA_d = din("maskA", [128, 5 * 8 * 128])
    hv_d = din("hv", [128, 4])
    lamp_d = din("lamp", [128, 256])
    subg_d = din("subg", [128, 1])
    w_ba = din("w_ba", [512, D])
    w_bb = din("w_bb", [512, D])
    w_out = din("w_out", [D, D])
    ln1g_d = din("ln1g", [128, 8])
    ln1b_d = din("ln1b", [128, 8])
    ln2g_d = din("ln2g", [128, 8])
    ln2b_d = din("ln2b", [128, 8])
    w_ff1 = din("w_ff1", [D, 4096])
    w_ff2 = din("w_ff2", [4096, D])
    outT = nc.dram_tensor("outT", [D, NTOK], F32, kind="ExternalOutput").ap()
    if DEBUG:
        dbg_ob = nc.dram_tensor("dbg_ob", [128, 4 * NTOK], BF16, kind="ExternalOutput").ap()
        dbg_oa = nc.dram_tensor("dbg_oa", [64, 8 * 512], BF16, kind="ExternalOutput").ap()
        dbg_x1 = nc.dram_tensor("dbg_x1", [128, 8 * 512], F32, kind="ExternalOutput").ap()
    s_xown = nc.dram_tensor("s_xown", [D, NTOK], BF16).ap()
    s_xhalo = nc.dram_tensor("s_xhalo", [D, NTOK], BF16).ap()
    s_win = nc.dram_tensor("s_win", [D, 5120], BF16).ap()
    s_ba = nc.dram_tensor("s_ba", [512, D], BF16).ap()
    s_bb = nc.dram_tensor("s_bb", [512, D], BF16).ap()
    s_out = nc.dram_tensor("s_out", [D, D], BF16).ap()
    s_ff1 = nc.dram_tensor("s_ff1", [D, 4096], BF16).ap()
    s_ff2 = nc.dram_tensor("s_ff2", [4096, D], BF16).ap()

    with contextlib.ExitStack() as top:
        P = Prog(nc, top)
        b = P.b

        def sbt(stack, name, shape, dt):
            return stack.enter_context(nc.sbuf_tensor("sb_" + name, shape, dt))

        PS = [top.enter_context(nc.psum_tensor("ps%d" % i, [128, 1024], F32)) for i in range(4)]
        pring = [0]

        def bankbufs(i, half=None):
            if half is None:
                return [b("bank", 2 * i), b("bank", 2 * i + 1)]
            return [b("bank", 2 * i + half)]

        ident_b = sbt(top, "ident_b", [128, 128], BF16)
        ones_b = sbt(top, "ones_b", [128, 128], BF16)
        ones_f = sbt(top, "ones_f", [128, 128], F32)
        negm = sbt(top, "negm", [128, 128], BF16)
        eps_t = sbt(top, "eps_t", [128, 1], F32)
        neg_lam = sbt(top, "neg_lam", [128, 1], F32)
        gsc = sbt(top, "gsc", [128, 1], F32)
        bgate = sbt(top, "bgate", [128, 16], F32)
        ln1g = sbt(top, "ln1g", [128, 8], F32)
        ln1b = sbt(top, "ln1b", [128, 8], F32)
        ln2g = sbt(top, "ln2g", [128, 8], F32)
        ln2b = sbt(top, "ln2b", [128, 8], F32)
        hv = sbt(top, "hv", [128, 4], F32)
        o_bT = sbt(top, "o_bT", [128, 4, NTOK], BF16)

        d_out = P.dsem("out")
        swc = [0]

        def swdma(dst, src, bufs):
            swc[0] += 1
            P.dma("gpsimd", P.dsem("sw%d" % swc[0], bg=True), dst, src, writes=bufs)

        swdma(s_win[:, 1536:3072], w_in[:, 1536:3072], [b("scr_wkvq")])
        for cb in range(2):
            swdma(s_xown[:, cb * 1024:(cb + 1) * 1024], xT_own[:, cb * 1024:(cb + 1) * 1024], [b("scr_xown", cb)])
        for cb in range(2):
            swdma(s_xhalo[:, cb * 1024:(cb + 1) * 1024], xT_halo[:, cb * 1024:(cb + 1) * 1024], [b("scr_xhalo", cb)])
        for r0 in range(0, D, 256):
            swdma(s_win[r0:r0 + 256, 0:1536], w_in[r0:r0 + 256, 0:1536], [b("scr_winA", r0)])
        swdma(s_ba, w_ba, [b("scr_ba")])
        swdma(s_bb, w_bb, [b("scr_bb")])
        for r0 in range(0, D, 256):
            swdma(s_win[r0:r0 + 256, 3072:5120], w_in[r0:r0 + 256, 3072:5120], [b("scr_winG", r0)])
        for r0 in range(0, D, 512):
            swdma(s_out[r0:r0 + 512, :], w_out[r0:r0 + 512, :], [b("scr_out", r0)])
        for r0 in range(0, D, 128):
            swdma(s_ff1[r0:r0 + 128, :], w_ff1[r0:r0 + 128, :], [b("scr_ff1", r0)])
        for r0 in range(0, 4096, 512):
            swdma(s_ff2[r0:r0 + 512, :], w_ff2[r0:r0 + 512, :], [b("scr_ff2", r0)])
        R_WINA = [b("scr_winA", r0) for r0 in range(0, D, 256)]
        R_WING = [b("scr_winG", r0) for r0 in range(0, D, 256)]
        R_OUT = [b("scr_out", r0) for r0 in range(0, D, 512)]
        R_FF1 = [b("scr_ff1", r0) for r0 in range(0, D, 128)]

        ccount = [0]

        def cdma(queue, dst, src, bufs):
            ccount[0] += 1
            P.dma(queue, P.dsem("c%d" % ccount[0]), dst, src, writes=bufs)

        cdma("sync", ident_b[:], ident_d, [b("c_ident")])
        for i, (dst, src) in enumerate(((bgate, bgate_d), (ln1g, ln1g_d), (ln1b, ln1b_d), (ln2g, ln2g_d),
                                        (ln2b, ln2b_d), (hv, hv_d), (gsc, subg_d))):
            cdma("sync", dst[:], src, [b("c_small", i)])
        P.dve(lambda e: e.memset(ones_b[:], 1.0), writes=[b("c_ones_b")])
        P.dve(lambda e: e.memset(ones_f[:], 1.0), writes=[b("c_ones_f")])
        P.dve(lambda e: e.memset(negm[:, 0:64], 0.0), writes=[b("c_negm")])
        P.dve(lambda e: e.memset(negm[:, 64:128], -30000.0), writes=[b("c_negm")])
        P.dve(lambda e: e.memset(eps_t[:], EPS), writes=[b("c_eps")])
        P.dve(lambda e: e.tensor_scalar(gsc[:], gsc[:], 1.0 - LAM_INIT, None, ALU.mult),
              reads=[b("c_small", 6)], writes=[b("c_small", 6)])

        with contextlib.ExitStack() as sAB:
            K_bT = sbt(sAB, "K_bT", [128, 4, S], BF16)
            V_b = sbt(sAB, "V_b", [128, 64, 512], BF16)
            Q_bT = sbt(sAB, "Q_bT", [128, 4, NTOK], BF16)

            with contextlib.ExitStack() as sA:
                cs_all = [sbt(sA, "cos_all", [128, 64 * 8], F32), sbt(sA, "sin_all", [128, 64 * 8], F32)]
                cs_own = [sbt(sA, "cos_own", [128, 16 * 8], F32), sbt(sA, "sin_own", [128, 16 * 8], F32)]
                w_in_v = w_in.rearrange("(c p) n -> p c n", p=128)

                with contextlib.ExitStack() as s0:
                    lamp = sbt(s0, "lamp", [128, 256], F32)
                    lpr = sbt(s0, "lpr", [128, 128], F32)
                    ls = sbt(s0, "ls", [128, 2], F32)
                    posi = sbt(s0, "posi", [128, 80], I32)
                    posf = sbt(s0, "posf", [128, 80], F32)
                    invf = sbt(s0, "invf", [128, 8], F32)
                    ang = sbt(s0, "ang", [128, 640], F32)
                    a2 = sbt(s0, "a2", [128, 640], F32)
                    tf = sbt(s0, "tf", [128, 640], F32)
                    ti = sbt(s0, "ti", [128, 640], I32)
                    t0 = [b("p0")]
                    cdma("sync", lamp[:], lamp_d, [b("p0_lamp")])
                    cdma("sync", posi[:, 0:64], pos_all, [b("p0_pa")])
                    cdma("sync", posi[:, 64:80], pos_qb, [b("p0_po")])
                    cdma("sync", invf[:], invf_d, [b("p0_invf")])
                    P.dve(lambda e: e.memset(ls[:], 0.0), reads=[b("p0_lamp"), b("p0_pa"), b("p0_po"), b("p0_invf")], writes=t0)
                    P.dve(lambda e: e.tensor_tensor(lpr[:, 0:64], lamp[:, 0:64], lamp[:, 64:128], ALU.mult), t0, t0)
                    P.dve(lambda e: e.tensor_tensor(lpr[:, 64:128], lamp[:, 128:192], lamp[:, 192:256], ALU.mult), t0, t0)
                    P.dve(lambda e: e.reduce_sum(ls[:, 0:1], lpr[:, 0:64], AX.X), t0, t0)
                    P.dve(lambda e: e.reduce_sum(ls[:, 1:2], lpr[:, 64:128], AX.X), t0, t0)
                    P.act(lambda e: e.activation(ls[:], ls[:], AF.Exp), t0, t0)
                    P.dve(lambda e: e.tensor_tensor(neg_lam[:], ls[:, 1:2], ls[:, 0:1], ALU.subtract), t0, t0 + [b("consts")])
                    P.dve(lambda e: e.tensor_scalar(neg_lam[:], neg_lam[:], -LAM_INIT, None, ALU.add), t0, t0 + [b("consts")])
                    P.dve(lambda e: e.tensor_copy(posf[:], posi[:]), t0, t0)
                    P.dve(lambda e: e.tensor_tensor(
                        ang[:].rearrange("p (t f) -> p t f", f=8),
                        posf[:].unsqueeze(2).to_broadcast([128, 80, 8]),
                        invf[:].unsqueeze(1).to_broadcast([128, 80, 8]), ALU.mult), t0, t0)
                    C1 = 6.28125
                    C2 = 2.0 * math.pi - C1
                    for which, shift in ((0, math.pi / 2.0), (1, 0.0)):
                        P.dve(lambda e, shift=shift: e.tensor_scalar(a2[:], ang[:], shift, None, ALU.add), t0, t0)
                        P.dve(lambda e: e.tensor_scalar(tf[:], a2[:], 1.0 / (2.0 * math.pi), None, ALU.mult), t0, t0)
                        P.dve(lambda e: e.tensor_copy(ti[:], tf[:]), t0, t0)
                        P.dve(lambda e: e.tensor_copy(tf[:], ti[:]), t0, t0)
                        P.dve(lambda e: e.scalar_tensor_tensor(a2[:], tf[:], -C1, a2[:], ALU.mult, ALU.add), t0, t0)
                        P.dve(lambda e: e.scalar_tensor_tensor(a2[:], tf[:], -C2, a2[:], ALU.mult, ALU.add), t0, t0)
                        P.dve(lambda e: e.tensor_scalar(tf[:], a2[:], math.pi, -2.0 * math.pi, ALU.is_gt, ALU.mult), t0, t0)
                        P.dve(lambda e: e.tensor_tensor(a2[:], a2[:], tf[:], ALU.add), t0, t0)
                        P.dve(lambda e: e.tensor_scalar(tf[:], a2[:], -math.pi, 2.0 * math.pi, ALU.is_lt, ALU.mult), t0, t0)
                        P.dve(lambda e: e.tensor_tensor(a2[:], a2[:], tf[:], ALU.add), t0, t0)
                        P.act(lambda e, which=which: e.activation(cs_all[which][:], a2[:, 0:512], AF.Sin), t0, t0 + [b("cs")])
                        P.act(lambda e, which=which: e.activation(cs_own[which][:], a2[:, 512:640], AF.Sin), t0, t0 + [b("cs")])
                    P.barrier()
                    P.flush()
                    if STOP == 0:
                        return nc

                wkv = sbt(sA, "wkv", [128, 8, 1024], BF16)
                wqb = sbt(sA, "wqb", [128, 8, 512], BF16)
                xs = [sbt(sA, "xs%d" % i, [128, 8, 256], BF16) for i in range(2)]
                K_tm = [sbt(sA, "K_tm%d" % i, [128, 512], BF16) for i in range(2)]
                rt = [sbt(sA, "rt%d" % i, [128, 64], F32) for i in range(4)]
                d_xs = [P.dsem("xs%d" % i) for i in range(2)]
                d_w = [P.dsem("wA0"), P.dsem("wA1")]
                swin_v0 = s_win.rearrange("(c p) n -> p c n", p=128)
                P.dma("sync", d_w[0], wkv[:], swin_v0[:, :, 2048:3072], reads=[b("scr_wkvq")], writes=[b("wkv")])
                P.dma("sync", d_w[1], wqb[:], swin_v0[:, :, 1536:2048], reads=[b("scr_wkvq")], writes=[b("wqb")])

                def proj_rope_T(T, xsb, xbuf, tt, w, wbuf, ncol, cs, ci, dstT, v_dst):
                    pi = pring[0] % 2
                    pring[0] += 1
                    ps = PS[pi]
                    nh = ncol // 512
                    for half in range(nh):
                        for kc in range(8):
                            P.pe(lambda e, half=half, kc=kc: e.matmul(
                                ps[:, half * 512:(half + 1) * 512], xsb[:, kc, tt * 128:(tt + 1) * 128],
                                w[:, kc, half * 512:(half + 1) * 512], start=(kc == 0), stop=(kc == 7)),
                                reads=[xbuf, wbuf], writes=bankbufs(pi, half))
                    ktm = K_tm[T % 2]
                    kb = b("ktm", T % 2)
                    if v_dst is not None:
                        P.act(lambda e: e.copy(v_dst, ps[:, 512:1024]), reads=bankbufs(pi, 1), writes=[b("Vb", T)])
                    psv = ps[:, 0:512].rearrange("p (s d) -> p s d", d=64)
                    ktv = ktm[:].rearrange("p (s d) -> p s d", d=64)
                    P.dve(lambda e: e.tensor_copy(ktv[:, :, 16:64], psv[:, :, 16:64]), reads=bankbufs(pi, 0), writes=[kb])
                    cosb = cs[0][:, ci * 8:(ci + 1) * 8].unsqueeze(1).to_broadcast([128, 8, 8])
                    sinb = cs[1][:, ci * 8:(ci + 1) * 8].unsqueeze(1).to_broadcast([128, 8, 8])
                    r = [x[:].rearrange("p (s d) -> p s d", d=8) for x in rt]
                    rb = [b("rt")]
                    t1 = psv[:, :, 0:8]
                    t2 = psv[:, :, 8:16]
                    P.dve(lambda e: e.tensor_tensor(r[0], t1, cosb, ALU.mult), bankbufs(pi, 0) + [b("cs")], rb)
                    P.dve(lambda e: e.tensor_tensor(r[1], t2, sinb, ALU.mult), bankbufs(pi, 0) + [b("cs")], rb)
                    P.dve(lambda e: e.tensor_tensor(r[2], t1, sinb, ALU.mult), bankbufs(pi, 0) + [b("cs")], rb)
                    P.dve(lambda e: e.tensor_tensor(r[3], t2, cosb, ALU.mult), bankbufs(pi, 0) + [b("cs")], rb)
                    P.dve(lambda e: e.tensor_tensor(ktv[:, :, 0:8], r[0], r[1], ALU.subtract), rb, [kb])
                    P.dve(lambda e: e.tensor_tensor(ktv[:, :, 8:16], r[2], r[3], ALU.add), rb, [kb])
                    qi = 2 + (T % 2)
                    pt = PS[qi][:].bitcast(BF16)
                    for h in range(4):
                        P.pe(lambda e, h=h: e.transpose(pt[:, h * 128:(h + 1) * 128], ktm[:, h * 128:(h + 1) * 128],
                                                        ident_b[:]),
                             reads=[kb, b("consts")], writes=bankbufs(qi, 0))
                    P.act(lambda e: e.copy(dstT, pt[:, 0:512].rearrange("p (h t) -> p h t", h=4)),
                          reads=bankbufs(qi, 0), writes=[b("KT", T)])

                xall_v = xT_all.rearrange("(c p) t -> p c t", p=128)
                xqb_v = xT_qb.rearrange("(c p) t -> p c t", p=128)
                obf = o_bT[:].rearrange("p h t -> p (h t)")
                stg = [obf[:, i * 4096:(i + 1) * 4096].bitcast(F32).rearrange("p (c t) -> p c t", c=8) for i in range(2)]

                def load_xs(i):
                    if i < 32:
                        src = xall_v[:, :, i * 256:(i + 1) * 256]
                    elif i < 40:
                        src = xqb_v[:, :, (i - 32) * 256:(i - 31) * 256]
                    else:
                        return
                    P.dma("sync", d_xs[i % 2], stg[i % 2], src, writes=[b("stg", i % 2)])
                    P.act(lambda e, i=i: e.copy(xs[i % 2][:], stg[i % 2]), reads=[b("stg", i % 2)], writes=[b("xs", i % 2)])

                load_xs(0)
                load_xs(1)
                for blk in range(NBLK):
                    for tt in range(2):
                        T = blk * 2 + tt
                        if blk < 32:
                            proj_rope_T(T, xs[blk % 2], b("xs", blk % 2), tt, wkv, b("wkv"), 1024, cs_all, T,
                                        K_bT[:, :, T * 128:(T + 1) * 128], V_b[:, T, :])
                        else:
                            To = T - 64
                            proj_rope_T(T, xs[blk % 2], b("xs", blk % 2), tt, wqb, b("wqb"), 512, cs_own, To,
                                        Q_bT[:, :, To * 128:(To + 1) * 128], None)
                    if blk + 2 < NBLK:
                        load_xs(blk + 2)
                P.barrier()
                P.flush()
                if STOP == 1:
                    return nc

            with contextlib.ExitStack() as sB:
                Eb = [sbt(sB, "Eb%d" % i, [128, 1024], BF16) for i in range(3)]
                Osb = sbt(sB, "Osb", [128, 1024], F32)
                T1 = sbt(sB, "T1", [128, 512], F32)
                T2 = sbt(sB, "T2", [128, 512], F32)
                sqb = sbt(sB, "sqb", [128, 512], BF16)
                lnv = sbt(sB, "lnv", [128, 512], F32)
                zs = sbt(sB, "zs", [128, 1024], F32)
                ui = [0]
                ei = [0]

                for t in range(4):
                    low = (t < 2)
                    qs = slice(t * 512, (t + 1) * 512)
                    for h in range(4):
                        pend = {}

                        def v3q(ap2, kt):
                            return ap2[:, 8 * kt:512]

                        def emit_S(kt, h=h, qs=qs, low=low):
                            pi = ui[0] % 2
                            ui[0] += 1
                            ps = PS[pi]
                            ks = slice(kt * 128, (kt + 1) * 128)
                            for m in range(2):
                                pr = slice(m * 64, (m + 1) * 64)
                                P.pe(lambda e, m=m, pr=pr: e.matmul(
                                    v3q(ps[:, m * 512:(m + 1) * 512], kt), K_bT[pr, h, ks], v3q(Q_bT[pr, h, qs], kt),
                                    start=True, stop=True), writes=bankbufs(pi, m))
                            ek = ei[0] % 3
                            ei[0] += 1
                            E = Eb[ek]
                            E3 = E[:].rearrange("p (m c) -> p m c", m=2)
                            P3 = ps[:].rearrange("p (m c) -> p m c", m=2)
                            if low:
                                Ed = E[64:128, :].rearrange("p (m c) -> p m c", m=2)[:, :, 8 * kt:8 * kt + 8]
                                P.pool(lambda e: e.memset(Ed, 0.0), writes=[b("E", ek)])
                                if kt < 63:
                                    P.act(lambda e: e.activation(E3[:, :, 8 * kt + 8:512], P3[:, :, 8 * kt + 8:512], AF.Exp,
                                                                 scale=0.125), reads=bankbufs(pi), writes=[b("E", ek)])
                                P.act(lambda e: e.activation(E3[0:64, :, 8 * kt:8 * kt + 8], P3[0:64, :, 8 * kt:8 * kt + 8],
                                                             AF.Exp, scale=0.125), reads=bankbufs(pi), writes=[b("E", ek)])
                            else:
                                P.act(lambda e: e.activation(E3[:, :, 8 * kt:512], P3[:, :, 8 * kt:512], AF.Exp, scale=0.125),
                                      reads=bankbufs(pi), writes=[b("E", ek)])
                            pend[kt] = ek

                        def emit_PV(kt, h=h):
                            ek = pend.pop(kt)
                            E = Eb[ek]
                            first = (kt == 0)
                            last = (kt == 63)
                            vv = V_b[:, kt, h * 128:(h + 1) * 128]
                            for m in range(2):
                                rhs = v3q(E[:, m * 512:(m + 1) * 512], kt)
                                P.pe(lambda e, m=m, rhs=rhs: e.matmul(
                                    v3q(PS[2][:, m * 512:(m + 1) * 512], kt), vv, rhs, start=first, stop=last),
                                    reads=[b("E", ek)], writes=bankbufs(2, m))
                                P.pe(lambda e, m=m, rhs=rhs: e.matmul(
                                    v3q(PS[3][:, m * 512:(m + 1) * 512], kt), ones_b[:, :], rhs, start=first, stop=last),
                                    reads=[b("E", ek)], writes=bankbufs(3, m))

                        emit_S(0)
                        emit_S(1)
                        for kt in range(64):
                            if kt + 2 < 64:
                                emit_S(kt + 2)
                            emit_PV(kt)
                        fb = [b("fin")]
                        P.act(lambda e: e.copy(zs[:], PS[3][:]), reads=bankbufs(3), writes=fb)
                        P.act(lambda e: e.copy(Osb[:], PS[2][:]), reads=bankbufs(2), writes=[b("Osb")])
                        P.dve(lambda e: e.reciprocal(zs[:], zs[:]), fb, fb)
                        P.dve(lambda e: e.tensor_tensor(T1[:], Osb[:, 0:512], zs[:, 0:512], ALU.mult),
                              reads=fb + [b("Osb")], writes=[b("T1")])
                        P.dve(lambda e: e.scalar_tensor_tensor(T2[:], Osb[:, 512:1024], neg_lam[:, 0:1], zs[:, 512:1024],
                                                               ALU.mult, ALU.mult),
                              reads=fb + [b("Osb")], writes=[b("T2")])
                        P.dve(lambda e: e.tensor_tensor(T1[:], T1[:], T2[:], ALU.add),
                              reads=[b("T1"), b("T2")], writes=[b("T1")])
                        P.act(lambda e: e.activation(sqb[:], T1[:], AF.Square), reads=[b("T1")], writes=[b("sqb")])
                        pj = ui[0] % 2
                        ui[0] += 1
                        ps2 = PS[pj]
                        P.pe(lambda e, ps2=ps2: e.matmul(ps2[:, 0:512], ones_b[:], sqb[:], start=True, stop=True),
                             reads=[b("sqb")], writes=bankbufs(pj, 0))
                        P.act(lambda e, ps2=ps2: e.activation(lnv[:], ps2[:, 0:512], AF.Ln, bias=eps_t[:], scale=1.0 / 128.0),
                              reads=bankbufs(pj, 0), writes=[b("lnv")])
                        P.act(lambda e: e.activation(lnv[:], lnv[:], AF.Exp, scale=-0.5), reads=[b("lnv")], writes=[b("lnv")])
                        P.dve(lambda e, h=h, qs=qs: e.scalar_tensor_tensor(
                            o_bT[:, h, qs].rearrange("p (a c) -> p c a", a=8),
                            T1[:].rearrange("p (c a) -> p c a", a=8), gsc[:, 0:1],
                            lnv[:].rearrange("p (c a) -> p c a", a=8), ALU.mult, ALU.mult),
                              reads=[b("T1"), b("lnv")], writes=[b("obT", t)])
                if DEBUG:
                    P.dma("sync", d_out, dbg_ob, o_bT[:].rearrange("p h t -> p (h t)"),
                          reads=[b("obT", t) for t in range(4)])
                P.barrier()
                P.flush()
                if STOP == 2:
                    return nc

        with contextlib.ExitStack() as sC:
            expB = sbt(sC, "expB", [128, 5, 1024], BF16)
            with contextlib.ExitStack() as sE:
                bf_ = sbt(sE, "bias_f", [128, 5120], F32)
                mf_ = sbt(sE, "mask_f", [128, 5120], F32)
                cdma("sync", bf_[:], biasT_d, [b("bf")])
                cdma("sync", mf_[:], maskA_d, [b("mf")])
                P.act(lambda e: e.activation(bf_[:], bf_[:], AF.Exp), reads=[b("bf")], writes=[b("bf")])
                P.dve(lambda e: e.tensor_tensor(expB[:].rearrange("p r n -> p (r n)"), bf_[:], mf_[:], ALU.mult),
                      reads=[b("bf"), b("mf")], writes=[b("expB")])
                P.barrier()
                P.flush()

            ring = [sbt(sC, "wr%d" % i, [128, 8192], BF16) for i in range(4)]
            d_ring = [P.dsem("wr%d" % i) for i in range(4)]
            xo_b = sbt(sC, "xo_b", [128, 8, 512], BF16)
            xhm = sbt(sC, "xhm", [128, 8, 512], BF16)
            xh_b = xhm
            mrg = xhm
            y1 = sbt(sC, "y1", [128, 8, 512], F32)
            Q_aT = sbt(sC, "Q_aT", [128, 4, 512], BF16)
            K_aT = sbt(sC, "K_aT", [128, 4, 1024], BF16)
            V_a = sbt(sC, "V_a", [128, 8, 8, 65], BF16)
            sig = [sbt(sC, "sig%d" % i, [128, 512], F32) for i in range(4)]
            EA4 = sbt(sC, "EA4", [128, 1024], BF16)
            EA = [sig[i][:].bitcast(BF16) for i in range(4)] + [EA4[:]]
            eab = [b("sig", i) for i in range(4)] + [b("EA4")]
            o_aT = sbt(sC, "o_aT", [64, 8, 512], BF16)
            zsa = sbt(sC, "zsa", [65, 1024], F32)
            OAs = sbt(sC, "OAs", [64, 1024], F32)
            x1b = sbt(sC, "x1b", [128, 8, 512], BF16)
            hT = sbt(sC, "hT", [128, 16, 512], BF16)
            ysq = hT
            st = [sbt(sC, "st%d" % i, [128, 512], F32) for i in range(3)]
            d_x = [P.dsem("xC%d" % i) for i in range(3)]
            P.dve(lambda e: e.memset(V_a[:, :, :, 64:65], 1.0), writes=[b("Va_ones")])

            swin_v = s_win.rearrange("(c p) n -> p c n", p=128)
            units = {}
            uorder = []
            ustate = {"nxt": 0}
            released = set()

            def unit(key, parts, rbufs):
                uorder.append((key, parts, rbufs))

            def pump():
                while ustate["nxt"] < len(uorder):
                    i = ustate["nxt"]
                    if i >= 4 and uorder[i - 4][0] not in released:
                        break
                    key, parts, rbufs = uorder[i]
                    ri = i % 4
                    for dst_fn, src in parts:
                        P.dma("sync", d_ring[ri], dst_fn(ring[ri]), src, reads=rbufs, writes=[b("ring", ri)])
                    units[key] = ri
                    ustate["nxt"] = i + 1

            def use(key):
                assert key in units, key
                return units[key]

            def release(key):
                released.add(key)
                pump()

            def v3(n):
                return lambda r: r[:, 0:8 * n].rearrange("p (c n) -> p c n", c=8)

            def gsel(gi):
                return lambda r: r[:, 0:8192].rearrange("p (c g n) -> p c g n", c=8, g=2)[:, :, gi, :]

            for t in range(4):
                unit((t, "qk"), [(v3(1024), swin_v[:, :, 0:1024])], R_WINA)
                unit((t, "v"), [(v3(512), swin_v[:, :, 1024:1536])], R_WINA)
                unit((t, "ba"), [(lambda r: r[0:64, 0:8192].rearrange("p (h n) -> p h n", h=8),
                                  s_ba.rearrange("(h p) n -> p h n", p=64))], [b("scr_ba")])
                unit((t, "bb"), [(lambda r: r[:, 0:4096].rearrange("p (h n) -> p h n", h=4),
                                  s_bb.rearrange("(h p) n -> p h n", p=128))], [b("scr_bb")])
                for mm in range(2):
                    unit((t, "gab", mm), [
                        (gsel(0), swin_v[:, :, 3072 + 512 * mm:3072 + 512 * (mm + 1)]),
                        (gsel(1), swin_v[:, :, 4096 + 512 * mm:4096 + 512 * (mm + 1)])], R_WING)
                unit((t, "out"), [(v3(1024), s_out.rearrange("(c p) n -> p c n", p=128))], R_OUT)
                for half in range(2):
                    for g in (2 * half, 2 * half + 1):
                        unit((t, "f1", g), [(v3(1024), s_ff1.rearrange("(c p) n -> p c n", p=128)[:, :, g * 1024:(g + 1) * 1024])], R_FF1)
                    for g in (2 * half, 2 * half + 1):
                        unit((t, "f2", g), [(v3(1024), s_ff2[g * 1024:(g + 1) * 1024, :].rearrange("(c p) n -> p c n", p=128))],
                             [b("scr_ff2", (g * 1024) // 512 * 512), b("scr_ff2", (g * 1024) // 512 * 512 + 512)])

            bk = [0]

            def next_bank():
                i = bk[0] % 8
                bk[0] += 1
                return PS[i // 2][:, (i % 2) * 512:(i % 2 + 1) * 512], [b("bank", i)]

            def next_pair():
                if bk[0] % 2:
                    bk[0] += 1
                i = (bk[0] % 8) // 2
                bk[0] += 2
                return PS[i], bankbufs(i)

            def layer_norm(g_t, b_t, want_bf16):
                yb = [b("y1", m) for m in range(8)]
                for m in range(8):
                    P.pool(lambda e, m=m: e.tensor_copy(x1b[:, m, :], y1[:, m, :]), reads=[yb[m]], writes=[b("x1b", m)])
                    P.act(lambda e, m=m: e.activation(ysq[:, m, :], y1[:, m, :], AF.Square), reads=[yb[m]],
                          writes=[b("hT", m)])
                ps_s, bs_s = next_bank()
                ps_q, bs_q = next_bank()
                for m in range(8):
                    P.pe(lambda e, m=m: e.matmul(ps_s, ones_b[:], x1b[:, m, :], start=(m == 0), stop=(m == 7)),
                         reads=[b("x1b", m)], writes=bs_s)
                for m in range(8):
                    P.pe(lambda e, m=m: e.matmul(ps_q, ones_b[:], ysq[:, m, :], start=(m == 0), stop=(m == 7)),
                         reads=[b("hT", m)], writes=bs_q)
                sb_ = [b("st")]
                P.act(lambda e: e.activation(st[0][:], ps_s, AF.Square, scale=1.0 / D), reads=bs_s, writes=sb_)
                P.dve(lambda e: e.scalar_tensor_tensor(st[1][:], ps_q, 1.0 / D, st[0][:], ALU.mult, ALU.subtract),
                      reads=bs_q + sb_, writes=sb_)
                P.act(lambda e: e.activation(st[1][:], st[1][:], AF.Ln, bias=eps_t[:], scale=1.0), sb_, sb_)
                P.act(lambda e: e.activation(st[1][:], st[1][:], AF.Exp, scale=-0.5), sb_, sb_)
                P.dve(lambda e: e.scalar_tensor_tensor(st[2][:], ps_s, -1.0 / D, st[1][:], ALU.mult, ALU.mult),
                      reads=bs_s + sb_, writes=sb_)
                for m in range(8):
                    P.pool(lambda e, m=m: e.tensor_tensor(y1[:, m, :], y1[:, m, :], st[1][:], ALU.mult),
                           reads=[yb[m]] + sb_, writes=[yb[m]])
                    P.pool(lambda e, m=m: e.tensor_tensor(y1[:, m, :], y1[:, m, :], st[2][:], ALU.add),
                           reads=[yb[m]] + sb_, writes=[yb[m]])
                    P.dve(lambda e, m=m: e.tensor_scalar(y1[:, m, :], y1[:, m, :], g_t[:, m:m + 1], b_t[:, m:m + 1],
                                                         ALU.mult, ALU.add), reads=[yb[m], b("consts")], writes=[yb[m]])
                    if want_bf16:
                        P.act(lambda e, m=m: e.copy(x1b[:, m, :], y1[:, m, :]), reads=[yb[m]], writes=[b("x1b", m)])

            xhalo_v = s_xhalo.rearrange("(c p) t -> p c t", p=128)
            xown_v = xT_own.rearrange("(c p) t -> p c t", p=128)
            xownb_v = s_xown.rearrange("(c p) t -> p c t", p=128)
            outT_v = outT.rearrange("(c p) t -> p c t", p=128)
            y1b = [b("y1", m) for m in range(8)]
            XHM = b("xhm")

            def do_slot(t):
                ts_ = slice(t * 512, (t + 1) * 512)
                P.dma("scalar", d_x[0], xo_b[:], xownb_v[:, :, ts_], reads=[b("scr_xown", t // 2)], writes=[b("xo")])
                P.dma("scalar", d_x[1], xh_b[:], xhalo_v[:, :, ts_], reads=[b("scr_xhalo", t // 2)], writes=[XHM])
                P.dma("scalar", d_x[2], y1[:], xown_v[:, :, ts_], writes=y1b)
                Wqk = v3(1024)(ring[use((t, "qk"))])
                wqk_b = [b("ring", use((t, "qk")))]
                Wv = v3(512)(ring[use((t, "v"))])
                wv_b = [b("ring", use((t, "v")))]
                for which, xsrc, xb_, dst, dbuf in (
                        (0, xo_b, b("xo"), lambda hp: Q_aT[:, hp, :], "QaT"),
                        (1, xh_b, XHM, lambda hp: K_aT[:, hp, 0:512], "KaT0"),
                        (1, xo_b, b("xo"), lambda hp: K_aT[:, hp, 512:1024], "KaT1")):
                    for hp in range(4):
                        ps, bs = next_bank()
                        c0 = which * 512 + hp * 128
                        for kc in range(8):
                            P.pe(lambda e, ps=ps, kc=kc, c0=c0, xsrc=xsrc: e.matmul(
                                ps, Wqk[:, kc, c0:c0 + 128], xsrc[:, kc, :], start=(kc == 0), stop=(kc == 7)),
                                reads=wqk_b + [xb_], writes=bs)
                        if hp % 2 == 0:
                            P.act(lambda e, ps=ps, hp=hp, dst=dst: e.copy(dst(hp), ps), reads=bs, writes=[b(dbuf, hp)])
                        else:
                            P.dve(lambda e, ps=ps, hp=hp, dst=dst: e.tensor_copy(dst(hp), ps), reads=bs,
                                  writes=[b(dbuf, hp)])
                release((t, "qk"))
                for si, (xsrc, xb_) in enumerate(((xh_b, XHM), (xo_b, b("xo")))):
                    for tt in range(4):
                        kt = si * 4 + tt
                        ps, bs = next_bank()
                        for kc in range(8):
                            P.pe(lambda e, ps=ps, kc=kc, tt=tt, xsrc=xsrc: e.matmul(
                                ps, xsrc[:, kc, tt * 128:(tt + 1) * 128], Wv[:, kc, :], start=(kc == 0), stop=(kc == 7)),
                                reads=wv_b + [xb_], writes=bs)
                        dstv = V_a[:, kt, :, 0:64]
                        srcv = ps.rearrange("p (h d) -> p h d", h=8)
                        if kt % 2 == 0:
                            P.act(lambda e, dstv=dstv, srcv=srcv: e.copy(dstv, srcv), reads=bs, writes=[b("Va", kt)])
                        else:
                            P.dve(lambda e, dstv=dstv, srcv=srcv: e.tensor_copy(dstv, srcv), reads=bs, writes=[b("Va", kt)])
                release((t, "v"))
                if STOP == 3:
                    raise StopBuild()
                P.dve(lambda e: e.tensor_scalar(
                    V_a[:, 0:4, :, 64:65], ones_b[:, 0:32].rearrange("p (a c o) -> p a c o", a=4, c=8),
                    hv[:, t:t + 1], None, ALU.mult),
                    reads=[b("consts")], writes=[b("Va_ones")])
                if STOP == 40:
                    raise StopBuild()
                for g in range(4):
                    gq = slice(g * 128, (g + 1) * 128)
                    for r in range(5):
                        kt = g + r
                        ps, bs = next_pair()
                        for h in range(8):
                            hp, e2 = divmod(h, 2)
                            pr = slice(e2 * 64, (e2 + 1) * 64)
                            hc = e2 * 512 + hp * 128
                            P.pe(lambda e, ps=ps, hc=hc, hp=hp, pr=pr, kt=kt, gq=gq: e.matmul(
                                ps[:, hc:hc + 128], K_aT[pr, hp, kt * 128:(kt + 1) * 128], Q_aT[pr, hp, gq],
                                start=True, stop=True),
                                reads=[b("QaT", hp), b("KaT0", hp), b("KaT1", hp)], writes=bs)
                        P.act(lambda e, ps=ps, r=r: e.activation(EA[r], ps[:], AF.Exp, scale=0.125),
                              reads=bs, writes=[eab[r]])
                        P.dve(lambda e, r=r: e.tensor_tensor(EA[r], EA[r], expB[:, r, :], ALU.mult),
                              reads=[eab[r], b("expB")], writes=[eab[r]])
                    if STOP == 41:
                        raise StopBuild()
                    po, bo = next_pair()
                    for h in range(8):
                        for r in range(5):
                            kt = g + r
                            hc = (h % 2) * 512 + (h // 2) * 128
                            P.pe(lambda e, h=h, r=r, kt=kt, po=po, hc=hc: e.matmul(
                                po[0:65, h * 128:(h + 1) * 128], V_a[:, kt, h, :], EA[r][:, hc:hc + 128],
                                start=(r == 0), stop=(r == 4)),
                                reads=[eab[r], b("Va", kt), b("Va_ones")], writes=bo)
                    if STOP == 42:
                        raise StopBuild()
                    zb = [b("zsa")]
                    P.act(lambda e, po=po: e.copy(zsa[64:65, :], po[64:65, :]), reads=bo, writes=zb)
                    P.act(lambda e, po=po: e.copy(OAs[:], po[0:64, :]), reads=bo, writes=[b("OAs")])
                    P.dve(lambda e: e.reciprocal(zsa[64:65, :], zsa[64:65, :]), zb, zb)
                    pz, bz = next_pair()
                    for hf in range(2):
                        P.pe(lambda e, hf=hf, pz=pz: e.matmul(
                            pz[0:64, hf * 512:(hf + 1) * 512], ones_f[64:65, 0:64], zsa[64:65, hf * 512:(hf + 1) * 512],
                            start=True, stop=True), reads=zb, writes=bz)
                    P.dve(lambda e, pz=pz, gq=gq: e.tensor_tensor(
                        o_aT[:, :, gq], OAs[:].rearrange("p (h q) -> p h q", h=8),
                        pz[0:64, :].rearrange("p (h q) -> p h q", h=8), ALU.mult),
                        reads=bz + [b("OAs")], writes=[b("oaT")])
                    if STOP == 43:
                        raise StopBuild()
                if DEBUG and t == 0:
                    P.dma("sync", d_out, dbg_oa, o_aT[:].rearrange("p h t -> p (h t)"), reads=[b("oaT")])
                if STOP == 4:
                    raise StopBuild()
                Wba = ring[use((t, "ba"))][0:64, 0:8192].rearrange("p (h n) -> p h n", h=8)
                wba_b = [b("ring", use((t, "ba")))]
                Wbb = ring[use((t, "bb"))][:, 0:4096].rearrange("p (h n) -> p h n", h=4)
                wbb_b = [b("ring", use((t, "bb")))]
                for m in range(8):
                    mm, ml = divmod(m, 4)
                    Wg = ring[use((t, "gab", mm))][:, 0:8192].rearrange("p (c g n) -> p c g n", c=8, g=2)
                    wg_b = [b("ring", use((t, "gab", mm)))]
                    pga, bga = next_bank()
                    pgb, bgb = next_bank()
                    for gi, (pg, bgx) in enumerate(((pga, bga), (pgb, bgb))):
                        for kc in range(8):
                            P.pe(lambda e, pg=pg, kc=kc, gi=gi, Wg=Wg, ml=ml: e.matmul(
                                pg, Wg[:, kc, gi, ml * 128:(ml + 1) * 128], xo_b[:, kc, :], start=(kc == 0), stop=(kc == 7)),
                                reads=wg_b + [b("xo")], writes=bgx)
                    sa = sig[(2 * m) % 4]
                    sb2 = sig[(2 * m + 1) % 4]
                    sab = [b("sig", (2 * m) % 4)]
                    sbb = [b("sig", (2 * m + 1) % 4)]
                    P.act(lambda e, pga=pga, sa=sa, m=m: e.activation(sa[:], pga, AF.Sigmoid, bias=bgate[:, m:m + 1]),
                          reads=bga + [b("consts")], writes=sab)
                    P.act(lambda e, pgb=pgb, sb2=sb2, m=m: e.activation(sb2[:], pgb, AF.Sigmoid, bias=bgate[:, 8 + m:9 + m]),
                          reads=bgb + [b("consts")], writes=sbb)
                    pba, bba = next_bank()
                    pbb, bbb = next_bank()
                    for h in range(8):
                        P.pe(lambda e, h=h, pba=pba, m=m: e.matmul(
                            pba, Wba[:, h, m * 128:(m + 1) * 128], o_aT[:, h, :], start=(h == 0), stop=(h == 7)),
                            reads=wba_b + [b("oaT")], writes=bba)
                    for hb in range(4):
                        P.pe(lambda e, hb=hb, pbb=pbb, m=m: e.matmul(
                            pbb, Wbb[:, hb, m * 128:(m + 1) * 128], o_bT[:, hb, ts_], start=(hb == 0), stop=(hb == 3)),
                            reads=wbb_b, writes=bbb)
                    P.dve(lambda e, sa=sa, pba=pba: e.tensor_tensor(sa[:], sa[:], pba, ALU.mult), reads=sab + bba, writes=sab)
                    P.dve(lambda e, sb2=sb2, pbb=pbb: e.tensor_tensor(sb2[:], sb2[:], pbb, ALU.mult), reads=sbb + bbb, writes=sbb)
                    P.pool(lambda e, sa=sa, sb2=sb2, m=m: e.tensor_tensor(mrg[:, m, :], sa[:], sb2[:], ALU.add),
                           reads=sab + sbb, writes=[XHM])
                    if ml == 3:
                        release((t, "gab", mm))
                release((t, "ba"))
                release((t, "bb"))
                if STOP == 5:
                    raise StopBuild()
                Wo = v3(1024)(ring[use((t, "out"))])
                wo_b = [b("ring", use((t, "out")))]
                for m in range(8):
                    ps, bs = next_bank()
                    for kc in range(8):
                        P.pe(lambda e, ps=ps, kc=kc, m=m: e.matmul(
                            ps, Wo[:, kc, m * 128:(m + 1) * 128], mrg[:, kc, :], start=(kc == 0), stop=(kc == 7)),
                            reads=wo_b + [XHM], writes=bs)
                    P.dve(lambda e, ps=ps, m=m: e.scalar_tensor_tensor(y1[:, m, :], y1[:, m, :], ALPHA, ps, ALU.mult, ALU.add),
                          reads=bs + [b("y1", m)], writes=[b("y1", m)])
                release((t, "out"))
                layer_norm(ln1g, ln1b, True)
                if DEBUG and t == 0:
                    P.dma("sync", d_out, dbg_x1, y1[:].rearrange("p c t -> p (c t)"), reads=y1b)
                if STOP == 6:
                    raise StopBuild()
                for half in range(2):
                    for fc in range(16):
                        f = half * 16 + fc
                        grp, wi = divmod(f, 8)
                        W1 = v3(1024)(ring[use((t, "f1", grp))])
                        w1_b = [b("ring", use((t, "f1", grp)))]
                        ps, bs = next_bank()
                        for kc in range(8):
                            P.pe(lambda e, ps=ps, kc=kc, wi=wi, W1=W1: e.matmul(
                                ps, W1[:, kc, wi * 128:(wi + 1) * 128], x1b[:, kc, :], start=(kc == 0), stop=(kc == 7)),
                                reads=w1_b + [b("x1b", kc)], writes=bs)
                        rl = sig[fc % 4]
                        rlb = [b("sig", fc % 4)]
                        P.act(lambda e, ps=ps, rl=rl: e.activation(rl[:], ps, AF.Relu), reads=bs, writes=rlb)
                        P.dve(lambda e, ps=ps, rl=rl, fc=fc: e.tensor_tensor(hT[:, fc, :], ps, rl[:], ALU.mult),
                              reads=bs + rlb, writes=[b("hT", fc)])
                        if wi == 7:
                            release((t, "f1", grp))
                    for m in range(8):
                        ps, bs = next_bank()
                        for fc in range(16):
                            f = half * 16 + fc
                            grp, wi = divmod(f, 8)
                            W2 = v3(1024)(ring[use((t, "f2", grp))])
                            w2_b = [b("ring", use((t, "f2", grp)))]
                            P.pe(lambda e, ps=ps, fc=fc, wi=wi, W2=W2, m=m: e.matmul(
                                ps, W2[:, wi, m * 128:(m + 1) * 128], hT[:, fc, :], start=(fc == 0), stop=(fc == 15)),
                                reads=w2_b + [b("hT", fc)], writes=bs)
                        if half == 0:
                            P.dve(lambda e, ps=ps, m=m: e.scalar_tensor_tensor(y1[:, m, :], y1[:, m, :], ALPHA, ps,
                                                                               ALU.mult, ALU.add),
                                  reads=bs + [b("y1", m)], writes=[b("y1", m)])
                        else:
                            P.dve(lambda e, ps=ps, m=m: e.tensor_tensor(y1[:, m, :], y1[:, m, :], ps, ALU.add),
                                  reads=bs + [b("y1", m)], writes=[b("y1", m)])
                    release((t, "f2", 2 * half))
                    release((t, "f2", 2 * half + 1))
                layer_norm(ln2g, ln2b, False)
                P.dma("sync", d_out, outT_v[:, :, ts_], y1[:], reads=y1b)

            pump()
            try:
                for t in range(4):
                    do_slot(t)
                    if STOP == 7:
                        raise StopBuild()
            except StopBuild:
                pass
            P.barrier()
            P.flush(final_wait=[d_out])

    return nc


def _own_blocks(j):
    return [j, 7 - j, 8 + j, 15 - j]


def _consts():
    p = np.arange(128)[:, None, None]
    r = np.arange(5)[None, :, None]
    q = np.arange(128)[None, None, :]
    kk = r * 128 + p
    dist = (512 + q) - kk
    idx = np.clip(dist, -128, 128) + 128
    kc = kk // 64
    qc = 8 + q // 64
    valid = (kc <= qc) & (kc >= qc - 8)
    return idx, valid


def prep_inputs(x, positions, w_in, b_gate, rel_bias, lambda_q1, lambda_k1, lambda_q2, lambda_k2, subln_g,
                w_branch_a, w_branch_b, w_out, ln1_g, ln1_b, w_ff1, w_ff2, ln2_g, ln2_b):
    x = np.asarray(x, dtype=np.float32)
    positions = np.asarray(positions)
    f32 = lambda a: np.ascontiguousarray(np.asarray(a, dtype=np.float32))
    idx, valid = _consts()
    rb = np.asarray(rel_bias, dtype=np.float32)[0]
    biasT = rb[:, idx]
    biasT = biasT[[0, 2, 4, 6, 1, 3, 5, 7]]
    biasT = np.ascontiguousarray(biasT.transpose(1, 2, 0, 3)).reshape(128, 5 * 8 * 128)
    maskA = np.broadcast_to(valid[:, :, None, :], (128, 5, 8, 128)).astype(np.float32).reshape(128, 5 * 8 * 128)
    invf = (np.float32(500000.0) ** (-np.arange(0, 16, 2, dtype=np.float32) / np.float32(16))).astype(np.float32)
    invf = np.ascontiguousarray(np.broadcast_to(invf[None, :], (128, 8)))
    ident = np.eye(128, dtype=np.float32).astype(ml_dtypes.bfloat16)
    pl = lambda v, c: np.ascontiguousarray(np.asarray(v, np.float32).reshape(c, 128).T)
    lamp = np.concatenate([np.asarray(a, np.float32).reshape(-1) for a in
                           (lambda_q1, lambda_k1, lambda_q2, lambda_k2)])[None, :]
    shared = {
        "invf": invf, "ident": ident, "w_in": f32(w_in[0]), "bgate": pl(b_gate[0], 16),
        "biasT": f32(biasT), "maskA": f32(maskA), "lamp": f32(np.broadcast_to(lamp, (128, 256))), "subg": f32(np.asarray(subln_g)[0].reshape(128, 1)),
        "w_ba": f32(w_branch_a[0]), "w_bb": f32(w_branch_b[0]), "w_out": f32(w_out[0]),
        "ln1g": pl(ln1_g[0], 8), "ln1b": pl(ln1_b[0], 8), "ln2g": pl(ln2_g[0], 8), "ln2b": pl(ln2_b[0], 8),
        "w_ff1": f32(w_ff1[0]), "w_ff2": f32(w_ff2[0]),
    }
    xT = [np.ascontiguousarray(x[bb].T) for bb in range(2)]
    in_maps = []
    own_tok = []
    for c in range(8):
        bi, j = divmod(c, 4)
        ob = _own_blocks(j)
        tok = np.concatenate([np.arange(o * 512, (o + 1) * 512) for o in ob])
        own_tok.append((bi, tok))
        xo = np.ascontiguousarray(xT[bi][:, tok])
        xh = np.zeros((D, NTOK), np.float32)
        hvv = np.ones((4,), np.float32)
        for s_, o in enumerate(ob):
            if o - 1 >= 0:
                xh[:, s_ * 512:(s_ + 1) * 512] = xT[bi][:, (o - 1) * 512:o * 512]
            else:
                hvv[s_] = 0.0
        qtok = np.empty((NTOK,), np.int64)
        qq = np.arange(8)[:, None]
        bb2 = np.arange(64)[None, :]
        for s_, o in enumerate(ob):
            qtok[s_ * 512:(s_ + 1) * 512] = (bb2 * 128 + 8 * o + qq).T.reshape(-1)
        m = dict(shared)
        m.update({
            "xT_all": xT[bi], "xT_own": xo, "xT_halo": xh,
            "pos_all": np.ascontiguousarray(positions[bi].astype(np.int32).reshape(64, 128).T),
            "xT_qb": np.ascontiguousarray(xT[bi][:, qtok]),
            "pos_qb": np.ascontiguousarray(positions[bi][qtok].astype(np.int32).reshape(16, 128).T),
            "hv": np.ascontiguousarray(np.broadcast_to(hvv[None, :], (128, 4))).astype(np.float32),
        })
        in_maps.append(m)
    return in_maps, own_tok


def kernel(**inputs):
    in_maps, own_tok = prep_inputs(**inputs)
    nc = build_nc()
    res = run_bass_kernel_spmd(nc, in_maps, core_ids=list(range(8)))
    out = np.empty((2, S, D), np.float32)
    for c in range(8):
        bi, tok = own_tok[c]
        out[bi, tok, :] = np.asarray(res.results[c]["outT"], dtype=np.float32).T
    if DEBUG:
        kernel.dbg = res.results
    return out
```
